# Optimizing a Trainium2 kernel written in Bass

```python
import jax, jax.numpy as jnp
from jax import lax
import numpy as np

D_MODEL = 1024
BATCH = 4
SEQ = 8192
DEPTH = 2
DEC_BATCH = 2
DEC_SEQ = 8192
PAST_LEN = 128

GRID_W = 64
N_HEADS_A = 8
N_KV_HEADS_A = 2
HEAD_DIM_A = 64
GROUP_A = N_HEADS_A // N_KV_HEADS_A
D_A = N_HEADS_A * HEAD_DIM_A
D_KV_A = N_KV_HEADS_A * HEAD_DIM_A
ROPE_THETA = 10000.0
ROPE_PAIRS = HEAD_DIM_A // 4
Q_BLOCK = 128
N_HEADS_B = 4
HEAD_DIM_B = 128
D_B = N_HEADS_B * HEAD_DIM_B
CHUNK_B = 64
QK_CONV_B = 3
N_GATES_B = 4 * N_HEADS_B
IN_COLS = D_A + 2 * D_KV_A + 4 * D_B + N_GATES_B
D_C = D_MODEL
CONV_C = 31
D_FF = 2816
CONV_FF = 3
N_EVEN = (DEPTH + 1) // 2
N_ODD = DEPTH // 2
EPS = 1e-6

kernel_name = "hybrid_bidir_attn_mlstm_conformer_convffn"


def rmsnorm(x, g):
    xf = x.astype(jnp.float32)
    y = xf * lax.rsqrt(jnp.mean(xf * xf, axis=-1, keepdims=True) + EPS)
    return (y * g.astype(jnp.float32)).astype(x.dtype)


def rms_f32(x, g):
    xf = x.astype(jnp.float32)
    return xf * lax.rsqrt(jnp.mean(xf * xf, axis=-1, keepdims=True) + EPS) * g.astype(jnp.float32)


def layernorm(x, g, b):
    xf = x.astype(jnp.float32)
    mu = jnp.mean(xf, axis=-1, keepdims=True)
    xc = xf - mu
    y = xc * lax.rsqrt(jnp.mean(xc * xc, axis=-1, keepdims=True) + EPS)
    return (y * g.astype(jnp.float32) + b.astype(jnp.float32)).astype(x.dtype)


def depthwise_conv(x, w):
    k, c = w.shape
    return lax.conv_general_dilated(
        x, w[:, None, :].astype(x.dtype), window_strides=(1,),
        padding=[(k // 2, k // 2)], dimension_numbers=("NWC", "WIO", "NWC"),
        feature_group_count=c)


def axial_rope(n):
    rows = n // GRID_W
    row_idx = jnp.repeat(jnp.arange(rows, dtype=jnp.float32), GRID_W)
    col_idx = jnp.tile(jnp.arange(GRID_W, dtype=jnp.float32), rows)
    inv = ROPE_THETA ** (-jnp.arange(ROPE_PAIRS, dtype=jnp.float32) / ROPE_PAIRS)
    ang = jnp.stack([row_idx[:, None] * inv, col_idx[:, None] * inv], axis=1)
    return jnp.cos(ang), jnp.sin(ang)


def apply_rope(x, cos, sin):
    shape = x.shape
    n = shape[1]
    xs = x.reshape(shape[:-1] + (2, 2, ROPE_PAIRS))
    x1, x2 = xs[..., 0, :], xs[..., 1, :]
    bshape = (n,) + (1,) * (x.ndim - 3) + (2, ROPE_PAIRS)
    c, s = cos.reshape(bshape), sin.reshape(bshape)
    out = jnp.stack([x1 * c - x2 * s, x2 * c + x1 * s], axis=-2)
    return out.reshape(shape)


def block_attention(q, k, v):
    bsz, n = q.shape[0], q.shape[1]
    nb = n // Q_BLOCK
    qb = q.reshape(bsz, nb, Q_BLOCK, N_KV_HEADS_A, GROUP_A, HEAD_DIM_A).transpose(1, 0, 2, 3, 4, 5)
    scale = HEAD_DIM_A ** -0.5

    def one_block(qi):
        s = jnp.einsum("bqhgd,bkhd->bhgqk", qi, k, preferred_element_type=jnp.float32) * scale
        p = jax.nn.softmax(s, axis=-1).astype(v.dtype)
        return jnp.einsum("bhgqk,bkhd->bqhgd", p, v)

    o = lax.map(one_block, qb)
    return o.transpose(1, 0, 2, 3, 4, 5).reshape(bsz, n, D_A)


def mlstm_scan(q, k, v, i_pre, f_pre):
    bsz, nh, n, dh = q.shape
    nc = n // CHUNK_B
    logf = jax.nn.log_sigmoid(f_pre)

    def chunks(t):
        t = t.reshape((bsz, nh, nc, CHUNK_B) + t.shape[3:])
        return jnp.moveaxis(t, 2, 0)

    qc, kc, vc, ic = chunks(q), chunks(k), chunks(v), chunks(i_pre)
    bc = jnp.cumsum(chunks(logf), axis=-1)
    tril = jnp.tril(jnp.ones((CHUNK_B, CHUNK_B), dtype=bool))

    def step(carry, inp):
        c_mat, n_vec, m = carry
        qj, kj, vj, ij, bj = inp
        log_d = bj[..., :, None] - bj[..., None, :] + ij[..., None, :]
        log_d = jnp.where(tril, log_d, -jnp.inf)
        inter = bj + m[..., None]
        m_row = jnp.maximum(jnp.max(log_d, axis=-1), inter)
        s = jnp.einsum("bhld,bhkd->bhlk", qj, kj) * jnp.exp(log_d - m_row[..., None])
        w_inter = jnp.exp(inter - m_row)
        num = (w_inter[..., None] * jnp.einsum("bhvk,bhlk->bhlv", c_mat, qj)
               + jnp.einsum("bhlk,bhkv->bhlv", s, vj))
        den = w_inter * jnp.einsum("bhk,bhlk->bhl", n_vec, qj) + jnp.sum(s, axis=-1)
        h = num / jnp.maximum(jnp.abs(den), jnp.exp(-m_row))[..., None]
        b_last = bj[..., -1]
        w_log = b_last[..., None] - bj + ij
        m_new = jnp.maximum(b_last + m, jnp.max(w_log, axis=-1))
        w = jnp.exp(w_log - m_new[..., None])
        decay = jnp.exp(b_last + m - m_new)
        c_new = decay[..., None, None] * c_mat + jnp.einsum("bhl,bhlv,bhlk->bhvk", w, vj, kj)
        n_new = decay[..., None] * n_vec + jnp.einsum("bhl,bhlk->bhk", w, kj)
        return (c_new, n_new, m_new), h

    init = (jnp.zeros((bsz, nh, dh, dh), jnp.float32),
            jnp.zeros((bsz, nh, dh), jnp.float32),
            jnp.zeros((bsz, nh), jnp.float32))
    _, h = lax.scan(step, init, (qc, kc, vc, ic, bc))
    return jnp.moveaxis(h, 0, 2).reshape(bsz, nh, n, dh)


def even_mixer(h, w_in, q_gain, k_gain, w_qk_conv, b_gates, h_gain, w_out):
    bsz, n, _ = h.shape
    proj = h @ w_in
    sizes = (D_A, D_KV_A, D_KV_A, D_B, D_B, D_B, D_B, N_GATES_B)
    idx = np.cumsum(sizes)[:-1].tolist()
    qa, ka, va, qb, kb, vb, ob, gates = jnp.split(proj, idx, axis=-1)

    cos, sin = axial_rope(n)
    qa = apply_rope(rms_f32(qa.reshape(bsz, n, N_KV_HEADS_A, GROUP_A, HEAD_DIM_A), q_gain), cos, sin).astype(h.dtype)
    ka = apply_rope(rms_f32(ka.reshape(bsz, n, N_KV_HEADS_A, HEAD_DIM_A), k_gain), cos, sin).astype(h.dtype)
    va = va.reshape(bsz, n, N_KV_HEADS_A, HEAD_DIM_A)
    out_a = block_attention(qa, ka, va)

    qk = jax.nn.silu(depthwise_conv(jnp.concatenate([qb, kb], axis=-1), w_qk_conv))
    qb, kb = jnp.split(qk, 2, axis=-1)

    def to_heads(t):
        return t.astype(jnp.float32).reshape(bsz, n, N_HEADS_B, HEAD_DIM_B).transpose(0, 2, 1, 3)

    q_h, k_h, v_h = to_heads(qb), to_heads(kb) * (HEAD_DIM_B ** -0.5), to_heads(vb)
    g = (gates.astype(jnp.float32) + b_gates.astype(jnp.float32))
    g = g.reshape(bsz, n, 4, N_HEADS_B).transpose(2, 0, 3, 1)

    def flip(t):
        return jnp.flip(t, axis=2)

    h_fwd = mlstm_scan(q_h, k_h, v_h, g[0], g[1])
    h_bwd = flip(mlstm_scan(flip(q_h), flip(k_h), flip(v_h), flip(g[2]), flip(g[3])))
    hb = (h_fwd + h_bwd).transpose(0, 2, 1, 3)
    hb = hb * lax.rsqrt(jnp.mean(hb * hb, axis=-1, keepdims=True) + EPS)
    hb = hb.reshape(bsz, n, D_B) * h_gain.astype(jnp.float32)
    out_b = (hb * jax.nn.sigmoid(ob.astype(jnp.float32))).astype(h.dtype)

    return jnp.concatenate([out_a, out_b], axis=-1) @ w_out


def conformer_conv(h, w_pw1, b_pw1, w_dw, b_dw, ln_g, ln_b, w_pw2, b_pw2):
    u = h @ w_pw1 + b_pw1
    a, gate = jnp.split(u, 2, axis=-1)
    u = a * jax.nn.sigmoid(gate)
    u = depthwise_conv(u, w_dw) + b_dw
    u = jax.nn.silu(layernorm(u, ln_g, ln_b))
    return u @ w_pw2 + b_pw2


def conv_ffn(h, w_up, w_dw, b_dw, w_down):
    u = depthwise_conv(h @ w_up, w_dw) + b_dw
    gate, val = jnp.split(u, 2, axis=-1)
    return (jax.nn.silu(gate) * val) @ w_down


def trunk(x, mix_norm_e, w_in, q_gain_a, k_gain_a, w_qk_conv_b, b_gates_b, h_gain_b, w_out_e,
          mix_norm_o, w_pw1_c, b_pw1_c, w_dw_c, b_dw_c, ln_g_c, ln_b_c, w_pw2_c, b_pw2_c,
          ffn_norm, w_up, w_dw_ff, b_dw_ff, w_down):
    for layer in range(DEPTH):
        j = layer // 2
        if layer % 2 == 0:
            x = x + even_mixer(rmsnorm(x, mix_norm_e[j]), w_in[j], q_gain_a[j], k_gain_a[j],
                               w_qk_conv_b[j], b_gates_b[j], h_gain_b[j], w_out_e[j])
        else:
            x = x + conformer_conv(rmsnorm(x, mix_norm_o[j]), w_pw1_c[j], b_pw1_c[j], w_dw_c[j],
                                   b_dw_c[j], ln_g_c[j], ln_b_c[j], w_pw2_c[j], b_pw2_c[j])
        x = x + conv_ffn(rmsnorm(x, ffn_norm[layer]), w_up[layer], w_dw_ff[layer],
                         b_dw_ff[layer], w_down[layer])
    return x


def setup_inputs(seed: int = 0) -> dict:
    key = jax.random.key(seed)
    ks = jax.random.split(key, 32)
    f32 = jnp.float32

    def nrm(k, shape, scale):
        return jax.random.normal(k, shape, f32) * scale

    def gain(k, shape):
        return 1.0 + nrm(k, shape, 0.02)

    f_bias = jnp.linspace(3.0, 6.0, N_HEADS_B, dtype=f32)
    zeros_h = jnp.zeros((N_HEADS_B,), f32)
    gate_base = jnp.stack([zeros_h, f_bias, zeros_h, f_bias], axis=0)
    b_gates_b = (gate_base[None] + nrm(ks[7], (N_EVEN, 4, N_HEADS_B), 0.1)).reshape(N_EVEN, N_GATES_B)

    return {
        "x_prompt": nrm(ks[0], (BATCH, SEQ, D_MODEL), 1.0),
        "x_sample": nrm(ks[1], (DEC_BATCH, DEC_SEQ, D_MODEL), 1.0),
        "mix_norm_e": gain(ks[2], (N_EVEN, D_MODEL)),
        "w_in": nrm(ks[3], (N_EVEN, D_MODEL, IN_COLS), D_MODEL ** -0.5),
        "q_gain_a": gain(ks[4], (N_EVEN, HEAD_DIM_A)),
        "k_gain_a": gain(ks[5], (N_EVEN, HEAD_DIM_A)),
        "w_qk_conv_b": nrm(ks[6], (N_EVEN, QK_CONV_B, 2 * D_B), QK_CONV_B ** -0.5),
        "b_gates_b": b_gates_b,
        "h_gain_b": gain(ks[8], (N_EVEN, D_B)),
        "w_out_e": nrm(ks[9], (N_EVEN, D_A + D_B, D_MODEL), (D_A + D_B) ** -0.5),
        "mix_norm_o": gain(ks[10], (N_ODD, D_MODEL)),
        "w_pw1_c": nrm(ks[11], (N_ODD, D_MODEL, 2 * D_C), D_MODEL ** -0.5),
        "b_pw1_c": nrm(ks[12], (N_ODD, 2 * D_C), 0.02),
        "w_dw_c": nrm(ks[13], (N_ODD, CONV_C, D_C), CONV_C ** -0.5),
        "b_dw_c": nrm(ks[14], (N_ODD, D_C), 0.02),
        "ln_g_c": gain(ks[15], (N_ODD, D_C)),
        "ln_b_c": nrm(ks[16], (N_ODD, D_C), 0.02),
        "w_pw2_c": nrm(ks[17], (N_ODD, D_C, D_MODEL), D_C ** -0.5),
        "b_pw2_c": nrm(ks[18], (N_ODD, D_MODEL), 0.02),
        "ffn_norm": gain(ks[19], (DEPTH, D_MODEL)),
        "w_up": nrm(ks[20], (DEPTH, D_MODEL, 2 * D_FF), D_MODEL ** -0.5),
        "w_dw_ff": nrm(ks[21], (DEPTH, CONV_FF, 2 * D_FF), CONV_FF ** -0.5),
        "b_dw_ff": nrm(ks[22], (DEPTH, 2 * D_FF), 0.02),
        "w_down": nrm(ks[23], (DEPTH, D_FF, D_MODEL), D_FF ** -0.5),
    }


def reference(x_prompt, x_sample, mix_norm_e, w_in, q_gain_a, k_gain_a, w_qk_conv_b, b_gates_b,
              h_gain_b, w_out_e, mix_norm_o, w_pw1_c, b_pw1_c, w_dw_c, b_dw_c, ln_g_c, ln_b_c,
              w_pw2_c, b_pw2_c, ffn_norm, w_up, w_dw_ff, b_dw_ff, w_down):
    weights = (mix_norm_e, w_in, q_gain_a, k_gain_a, w_qk_conv_b, b_gates_b, h_gain_b, w_out_e,
               mix_norm_o, w_pw1_c, b_pw1_c, w_dw_c, b_dw_c, ln_g_c, ln_b_c, w_pw2_c, b_pw2_c,
               ffn_norm, w_up, w_dw_ff, b_dw_ff, w_down)
    y_prompt = trunk(x_prompt, *weights)
    y_sample = trunk(x_sample, *weights)
    return (y_prompt, y_sample)
```

```python
import numpy as np
from contextlib import ExitStack
import concourse.bass as bass
import concourse.mybir as mybir
from concourse.bass_utils import run_bass_kernel_spmd

F32 = mybir.dt.float32
BF16 = mybir.dt.bfloat16
AF = mybir.ActivationFunctionType
ALU = mybir.AluOpType
AX = mybir.AxisListType

D = 1024
GRID_W = 64
EPS = 1e-6
D_FF = 2816
NCOL_IN = 3472
C_QA, C_QAR, C_KA, C_KAR, C_QB, C_KB, C_VA, C_G, C_VB, C_OB = 0, 512, 1024, 1152, 1280, 1792, 2304, 2432, 2448, 2960


class Op:
    __slots__ = ("eng", "fn", "deps", "sig", "ev", "dma", "dslot", "dprev")


class Sched:
    ENGS = ("pe", "act", "dve", "pool", "sp")
    EPOCH = 16000

    def __init__(self, nc, stack, n_dma=24, same_engine_sync=True):
        self.nc = nc
        self.stack = stack
        self.esem = {e: stack.enter_context(nc.semaphore("se_%s_0" % e)) for e in self.ENGS}
        self.eepoch = {e: 0 for e in self.ENGS}
        self.ecnt = {e: 0 for e in self.ENGS}
        self.dsem = [stack.enter_context(nc.semaphore("sd_%d" % i)) for i in range(n_dma)]
        self.dcnt = [0] * n_dma
        self.dnext = 0
        self.same = same_engine_sync
        self.embed = True
        self.waited = {e: {} for e in self.ENGS}
        self.reset()

    def reset(self):
        self.ops = []
        self.lastw = {}
        self.readers = {}

    def add(self, eng, fn, reads=(), writes=(), dma=False):
        op = Op()
        op.eng, op.fn, op.dma, op.sig, op.ev = eng, fn, dma, False, None
        excl = [r for r in reads if isinstance(r, tuple) and r[0] == "ps"]
        if excl:
            reads = [r for r in reads if r not in excl]
            writes = list(writes) + excl
        deps = set()
        for r in reads:
            w = self.lastw.get(r)
            if w is not None:
                deps.add(w)
        for r in writes:
            w = self.lastw.get(r)
            if w is not None:
                deps.add(w)
            for q in self.readers.get(r, ()):
                deps.add(q)
        deps.discard(op)
        op.deps = deps
        for r in reads:
            self.readers.setdefault(r, []).append(op)
        for r in writes:
            self.lastw[r] = op
            self.readers[r] = []
        if dma:
            i = self.dnext
            self.dnext = (self.dnext + 1) % len(self.dsem)
            op.dslot = i
            op.dprev = self.dcnt[i]
            self.dcnt[i] += 16
            op.ev = (self.dsem[i], self.dcnt[i])
        self.ops.append(op)
        return op

    def _needs_wait(self, op, d):
        if d.dma:
            return True
        if d.eng != op.eng:
            return True
        if op.eng == "pe" or op.dma:
            return False
        return self.same

    def emit(self, name):
        nc = self.nc
        ops = self.ops
        for op in ops:
            for d in op.deps:
                if not d.dma and self._needs_wait(op, d):
                    d.sig = True
        for op in ops:
            if op.dma or not op.sig:
                continue
            e = op.eng
            if self.ecnt[e] >= self.EPOCH:
                self.eepoch[e] += 1
                self.esem[e] = self.stack.enter_context(nc.semaphore("se_%s_%d" % (e, self.eepoch[e])))
                self.ecnt[e] = 0
            self.ecnt[e] += 1
            op.ev = (self.esem[e], self.ecnt[e])
        per = {e: [o for o in ops if o.eng == e] for e in self.ENGS}
        dsem_final = [(self.dsem[i], self.dcnt[i]) for i in range(len(self.dsem)) if self.dcnt[i] > 0]
        sched = self

        def body(ename):
            def f(eng):
                waited = sched.waited[ename]

                def wait(sem, val):
                    k = id(sem)
                    if waited.get(k, 0) >= val:
                        return
                    waited[k] = val
                    eng.wait_ge(sem, val)

                for op in per[ename]:
                    need = {}
                    for d in op.deps:
                        if sched._needs_wait(op, d):
                            s, v = d.ev
                            if need.get(id(s), (None, 0))[1] < v:
                                need[id(s)] = (s, v)
                    if op.dma and op.dprev > 0:
                        s = sched.dsem[op.dslot]
                        if need.get(id(s), (None, 0))[1] < op.dprev:
                            need[id(s)] = (s, op.dprev)
                    todo = [(s, v) for s, v in need.values() if waited.get(id(s), 0) < v]
                    emb = None
                    if todo and sched.embed:
                        emb = todo.pop()
                    for s, v in todo:
                        wait(s, v)
                    ins = op.fn(eng)
                    if emb is not None:
                        waited[id(emb[0])] = emb[1]
                        ins._wait_ge(emb[0], emb[1])
                    if op.dma:
                        ins.then_inc(sched.dsem[op.dslot], 16)
                    elif op.sig:
                        ins.then_inc(op.ev[0], 1)
                if ename == "sp":
                    for s, v in dsem_final:
                        wait(s, v)
            return f

        with nc.Block() as block:
            block.sync(body("sp"))
            block.tensor(body("pe"))
            block.scalar(body("act"))
            block.vector(body("dve"))
            block.gpsimd(body("pool"))
        self.reset()


class Prog:
    def __init__(self, S=8192, debug=()):
        self.S = S
        self.debug = set(debug)
        self.nc = bass.Bass("TRN2", target_bir_lowering=False)
        self.din = {}
        self.dout = {}

    def inp(self, name, shape, dt=F32):
        t = self.nc.dram_tensor(name, list(shape), dt, kind="ExternalInput").ap()
        self.din[name] = t
        return t

    def scratch(self, name, shape, dt):
        kind = "ExternalOutput" if name in self.debug else "Internal"
        t = self.nc.dram_tensor(name, list(shape), dt, kind=kind).ap()
        if name in self.debug:
            self.dout[name] = t
        return t

    def out(self, name, shape, dt=F32):
        t = self.nc.dram_tensor(name, list(shape), dt, kind="ExternalOutput").ap()
        self.dout[name] = t
        return t


def build(S=8192, debug=(), phases=("p1",), stop=99):
    P = Prog(S, debug)
    nc = P.nc
    NW = S // 512
    x = P.inp("x", [S, D])
    w_in = P.inp("w_in_ext", [D, NCOL_IN])
    cst = P.inp("cst", [128, 1024])
    ropeC = P.inp("ropeC", [128, S])
    ropeS = P.inp("ropeS", [128, S])
    vecs = P.inp("vecs", [128, 64])
    bgt = P.inp("bgt", [128, 16])
    V_GE, V_GQ, V_GQR, V_GK, V_GKR, V_WQK = 0, 8, 9, 10, 11, 12

    xT_s = P.scratch("xT_s", [D, S], F32)
    qaT_s = P.scratch("qaT_s", [512, S], BF16)
    kaT_s = P.scratch("kaT_s", [128, S], BF16)
    va_s = P.scratch("va_s", [S, 2 * 128], BF16)
    qbT_s = P.scratch("qbT_s", [512, S], BF16)
    kbT_s = P.scratch("kbT_s", [512, S], BF16)
    vb_s = P.scratch("vb_s", [S, 4 * 129], BF16)
    ob_s = P.scratch("ob_s", [S, 512], F32)
    g_s = P.scratch("g_s", [S, 16], F32)
    cat_s = P.scratch("cat_s", [S, 1024], BF16)
    catTa_s = P.scratch("catTa_s", [512, S], BF16)
    hf_s = P.scratch("hf_s", [S, 512], F32)
    hgt = P.inp("hgt", [128, 512])
    hb_s = P.scratch("hb_s", [S, 512], F32) if "hb_s" in debug else None
    xB_s = P.scratch("xB_s", [D, S], F32)
    w_out_d = P.inp("w_out", [D, D])
    w_up_d = [P.inp("w_up%d" % i, [D, 2 * D_FF]) for i in range(2)]
    w_down_d = [P.inp("w_down%d" % i, [D_FF, D]) for i in range(2)]
    w_pw1_d = P.inp("w_pw1", [D, 2 * D])
    w_pw2_d = P.inp("w_pw2", [D, D])
    vec3 = P.inp("vec3", [128, 704])
    y_d = P.out("y", [S, D])
    wo_b = P.scratch("wo_b", [D, D], BF16)
    wu_b = [P.scratch("wu_b%d" % i, [D, 2 * D_FF], BF16) for i in range(2)]
    wd_b = [P.scratch("wd_b%d" % i, [D_FF, D], BF16) for i in range(2)]
    w1_b = P.scratch("w1_b", [D, 2 * D], BF16)
    w2_b = P.scratch("w2_b", [D, D], BF16)
    V3_FN = (0, 8); V3_MO = 16; V3_BPW1 = 24; V3_WDC = 40; V3_BDC = 288; V3_LNG = 296; V3_LNB = 304; V3_BPW2 = 312
    V3_WFF = (320, 452); V3_BFF = (584, 628)

    with ExitStack() as G:
        _uq = [0]

        def uq(name):
            _uq[0] += 1
            return "%s_u%d" % (name, _uq[0])
        sb = lambda st, name, shape, dt=F32: st.enter_context(nc.sbuf_tensor(uq(name), list(shape), dt))
        Sx = Sched(nc, G)
        pctr = [0]

        def nextps():
            i = pctr[0] % 8
            pctr[0] += 1
            return i

        cst_t = sb(G, "cst_t", [128, 1024])
        vec_t = sb(G, "vec_t", [128, 64])
        bg_t = sb(G, "bg_t", [128, 16])
        cstb = sb(G, "cstb", [128, 384], BF16)
        ident = cst_t[:, 0:128]
        ones = cst_t[:, 128:256]
        blk64 = cst_t[:, 256:384]
        onesb = cstb[:, 128:256]
        blk64b = cstb[:, 256:384]

        def load_cst(A):
            A("sp", lambda e: e.dma_start(out=cst_t[:], in_=cst[:, :]), writes=["cst"], dma=True)
            A("dve", lambda e: e.tensor_copy(out=cstb[:], in_=cst_t[:, 0:384]), reads=["cst"], writes=["cstb"])

        if "p1" in phases:
            with ExitStack() as L:
                stage = [0]
                psb = [L.enter_context(nc.psum_tensor(uq("psb%d" % i), [128, 512], F32)) for i in range(8)]

                def A(*a, **k):
                    if stage[0] <= stop:
                        return Sx.add(*a, **k)
                W = sb(L, "W1", [128, 8, NCOL_IN], BF16)
                wst = [sb(L, "wst%d" % i, [128, NCOL_IN // 2]) for i in range(2)]
                xtok = [sb(L, "xtok%d" % i, [128, 4, D]) for i in range(1)] * 2
                xT = [sb(L, "xT%d" % i, [128, 8, 512]) for i in range(1)] * 2
                sq = sb(L, "sq", [128, 8, 512], BF16)
                rstd = sb(L, "rstd", [128, 512])
                lnt = sb(L, "lnt", [128, 512])
                xn = [sb(L, "xn%d" % i, [128, 8, 512], BF16) for i in range(2)]
                Ct = [sb(L, "Ct%d" % i, [128, 512]) for i in range(1)] * 2
                St = [sb(L, "St%d" % i, [128, 512]) for i in range(1)] * 2
                sqq = [sb(L, "sqq%d" % i, [128, 512], BF16) for i in range(2)]
                rq = [sb(L, "rq%d" % i, [128, 512]) for i in range(1)] * 2
                t1 = [sb(L, "t1_%d" % i, [128, 512]) for i in range(2)]
                t2 = [sb(L, "t2_%d" % i, [128, 512]) for i in range(2)]
                qst = [sb(L, "qst%d" % i, [128, 512], BF16) for i in range(4)]
                Ub = [sb(L, "Ub%d" % i, [128, 516]) for i in range(3)]
                tails = sb(L, "tails", [128, 8, 2])
                cv = [sb(L, "cv%d" % i, [128, 516]) for i in range(2)]
                sgp = [sb(L, "sgp%d" % i, [128, 516]) for i in range(2)]
                cst2 = [sb(L, "cst2_%d" % i, [128, 516], BF16) for i in range(4)]
                va_st = [sb(L, "va_st%d" % i, [128, 4, 2, 128], BF16) for i in range(2)]
                vb_st = [sb(L, "vb_st%d" % i, [128, 4, 4, 129], BF16) for i in range(2)]
                ob_st = [sb(L, "ob_st%d" % i, [128, 4, 512]) for i in range(1)] * 2
                g_st = [sb(L, "g_st%d" % i, [128, 4, 16]) for i in range(2)]

                load_cst(A)
                A("sp", lambda e: e.dma_start(out=vec_t[:], in_=vecs[:, :]), writes=["vec"], dma=True)
                A("sp", lambda e: e.dma_start(out=bg_t[:], in_=bgt[:, :]), writes=["bg"], dma=True)
                half = NCOL_IN // 2
                for c in range(8):
                    for h in range(2):
                        A("sp", lambda e, c=c, h=h: e.dma_start(out=wst[h][:], in_=w_in[c * 128:(c + 1) * 128, h * half:(h + 1) * half]),
                          writes=[("wst", h)], dma=True)
                    A("dve", lambda e, c=c: e.tensor_scalar(W[:, c, 0:half], wst[0][:],
                                                            vec_t[:, V_GE + c:V_GE + c + 1], None, ALU.mult),
                      reads=[("wst", 0), "vec"], writes=[("W", c, 0)])
                    A("pool", lambda e, c=c: e.tensor_scalar(W[:, c, half:], wst[1][:],
                                                             vec_t[:, V_GE + c:V_GE + c + 1], 1.0, ALU.mult, ALU.mult),
                      reads=[("wst", 1), "vec"], writes=[("W", c, 1)])
                Wres = [("W", c, h) for c in range(8) for h in range(2)]
                for s in range(2):
                    A("pool", lambda e, s=s: e.memset(va_st[s][:], 0.0), writes=[("va_st", s)])
                    A("pool", lambda e, s=s: e.memset(va_st[s][:, :, :, 64:65], 1.0), writes=[("va_st", s)])
                    A("pool", lambda e, s=s: e.memset(vb_st[s][:], 1.0), writes=[("vb_st", s)])
                for b in range(3):
                    A("pool", lambda e, b=b: e.memset(Ub[b][:], 0.0), writes=[("Ub", b)])
                A("pool", lambda e: e.memset(tails[:], 0.0), writes=["tails"])

                def p1_front(w):
                    a = w * 512
                    s = w % 2
                    last = (w == NW - 1)
                    stage[0] = 1
                    A("sp", lambda e, a=a, s=s: e.dma_start(
                        out=xtok[s][:], in_=x[a:a + 512, :].rearrange("(j p) d -> p j d", p=128)),
                      writes=[("xtok", 0)], dma=True)
                    A("sp", lambda e, a=a, s=s: e.dma_start(out=Ct[s][:], in_=ropeC[:, a:a + 512]),
                      writes=[("Ct", 0)], dma=True)
                    A("sp", lambda e, a=a, s=s: e.dma_start(out=St[s][:], in_=ropeS[:, a:a + 512]),
                      writes=[("St", 0)], dma=True)
                    for c in range(8):
                        pi = nextps()
                        for j in range(4):
                            A("pe", lambda e, pi=pi, j=j, c=c, s=s: e.transpose(
                                out=psb[pi][:, j * 128:(j + 1) * 128], in_=xtok[s][:, j, c * 128:(c + 1) * 128],
                                identity=ident), reads=[("xtok", 0), "cst"], writes=[("ps", pi)])
                        if c % 2 == 0:
                            A("act", lambda e, pi=pi, c=c, s=s: e.activation(out=xT[s][:, c, :], in_=psb[pi][:], func=AF.Copy),
                              reads=[("ps", pi)], writes=[("xT", 0, c)])
                        else:
                            A("dve", lambda e, pi=pi, c=c, s=s: e.tensor_copy(out=xT[s][:, c, :], in_=psb[pi][:]),
                              reads=[("ps", pi)], writes=[("xT", 0, c)])
                    xTres = [("xT", 0, c) for c in range(8)]
                    A("pool", lambda e, a=a, s=s: e.dma_start(
                        out=xT_s[:, a:a + 512].rearrange("(c p) t -> p c t", p=128), in_=xT[s][:]),
                      reads=xTres, dma=True)
                    stage[0] = 2
                    A("act", lambda e, s=s: e.activation(out=sq[:], in_=xT[s][:], func=AF.Square),
                      reads=xTres, writes=["sq"])
                def p1_front_b(w):
                    a = w * 512
                    s = w % 2
                    last = (w == NW - 1)
                    pi = nextps()
                    for c in range(8):
                        A("pe", lambda e, pi=pi, c=c: e.matmul(psb[pi][:], lhsT=onesb, rhs=sq[:, c, :],
                                                               start=(c == 0), stop=(c == 7)),
                          reads=["sq", "cstb"], writes=[("ps", pi)])
                    A("act", lambda e, pi=pi: e.activation(out=lnt[:], in_=psb[pi][:], func=AF.Ln, scale=1.0 / D, bias=EPS),
                      reads=[("ps", pi)], writes=["lnt"])
                    A("act", lambda e: e.activation(out=rstd[:], in_=lnt[:], func=AF.Exp, scale=-0.5),
                      reads=["lnt"], writes=["rstd"])
                    for c in range(8):
                        eng = "dve" if c % 2 == 0 else "pool"
                        A(eng, lambda e, c=c, s=s: e.tensor_tensor(out=xn[s][:, c, :], in0=xT[s][:, c, :], in1=rstd[:], op=ALU.mult),
                          reads=[("xT", 0, c), "rstd"], writes=[("xn", s, c)])
                    xnres = [("xn", s, c) for c in range(8)]

                def p1_part1(w):
                    a = w * 512
                    s = w % 2
                    last = (w == NW - 1)
                    def fm_mm(pi, col0, s=s):
                        for c in range(8):
                            A("pe", lambda e, pi=pi, c=c, col0=col0, s=s: e.matmul(
                                psb[pi][:], lhsT=W[:, c, col0:col0 + 128], rhs=xn[s][:, c, :],
                                start=(c == 0), stop=(c == 7)),
                              reads=[("xn", s, c), ("W", c, 0), ("W", c, 1)], writes=[("ps", pi)])

                    stage[0] = 3
                    hst = {}

                    def hp_a(hp):
                        colq = (C_QA + hp * 128) if hp < 4 else C_KA
                        colr = (C_QAR + hp * 128) if hp < 4 else C_KAR
                        b = hp % 2
                        pq = nextps()
                        fm_mm(pq, colq)
                        pr = nextps()
                        fm_mm(pr, colr)
                        A("act", lambda e, pq=pq, b=b: e.activation(out=sqq[b][:], in_=psb[pq][:], func=AF.Square),
                          reads=[("ps", pq)], writes=[("sqq", b)])
                        hst[hp] = (pq, pr)

                    def hp_b(hp):
                        pq, pr = hst[hp]
                        vg = V_GQ if hp < 4 else V_GK
                        vgr = V_GQR if hp < 4 else V_GKR
                        b = hp % 2
                        ph = nextps()
                        A("pe", lambda e, ph=ph, b=b: e.matmul(psb[ph][:], lhsT=blk64b, rhs=sqq[b][:], start=True, stop=True),
                          reads=[("sqq", b), "cstb"], writes=[("ps", ph)])
                        A("act", lambda e, ph=ph, b=b: e.activation(out=rq[b][:], in_=psb[ph][:], func=AF.Ln, scale=1.0 / 64, bias=EPS),
                          reads=[("ps", ph)], writes=[("rq", 0)])
                        A("act", lambda e, b=b: e.activation(out=rq[b][:], in_=rq[b][:], func=AF.Exp, scale=-0.5),
                          reads=[("rq", 0)], writes=[("rq", 0)])
                        stage[0] = 3.2
                        A("dve", lambda e, pq=pq, b=b, vg=vg, s=s: e.scalar_tensor_tensor(
                            out=t1[b][:], in0=psb[pq][:], scalar=vec_t[:, vg:vg + 1], in1=Ct[s][:], op0=ALU.mult, op1=ALU.mult),
                          reads=[("ps", pq), ("Ct", 0), "vec"], writes=[("t1", b)])
                        A("dve", lambda e, pr=pr, b=b, vgr=vgr, s=s: e.scalar_tensor_tensor(
                            out=t2[b][:], in0=psb[pr][:], scalar=vec_t[:, vgr:vgr + 1], in1=St[s][:], op0=ALU.mult, op1=ALU.mult),
                          reads=[("ps", pr), ("St", 0), "vec"], writes=[("t2", b)])
                        stage[0] = 3.3
                        A("pool", lambda e, b=b: e.tensor_tensor(out=t1[b][:], in0=t1[b][:], in1=t2[b][:], op=ALU.add),
                          reads=[("t1", b), ("t2", b)], writes=[("t1", b)])
                        qs = hp % 4
                        A("pool", lambda e, b=b, qs=qs: e.tensor_tensor(out=qst[qs][:], in0=t1[b][:], in1=rq[b][:], op=ALU.mult),
                          reads=[("t1", b), ("rq", 0)], writes=[("qst", qs)])
                        stage[0] = 3.4
                        dst = qaT_s[hp * 128:(hp + 1) * 128, a:a + 512] if hp < 4 else kaT_s[:, a:a + 512]
                        A("pool", lambda e, dst=dst, qs=qs: e.dma_start(out=dst, in_=qst[qs][:]),
                          reads=[("qst", qs)], dma=True)

                    hp_a(0)
                    for hp in range(5):
                        if hp + 1 < 5:
                            hp_a(hp + 1)
                        hp_b(hp)

                def p1_part2(w):
                    a = w * 512
                    s = w % 2
                    last = (w == NW - 1)
                    def fm_mm(pi, col0, s=s):
                        for c in range(8):
                            A("pe", lambda e, pi=pi, c=c, col0=col0, s=s: e.matmul(
                                psb[pi][:], lhsT=W[:, c, col0:col0 + 128], rhs=xn[s][:, c, :],
                                start=(c == 0), stop=(c == 7)),
                              reads=[("xn", s, c), ("W", c, 0), ("W", c, 1)], writes=[("ps", pi)])

                    stage[0] = 4
                    def ct_a(ct):
                        col = C_QB + ct * 128
                        pu = nextps()
                        fm_mm(pu, col)
                        u3 = ct % 3
                        A("act", lambda e, pu=pu, u3=u3: e.activation(out=Ub[u3][:, 2:514], in_=psb[pu][:], func=AF.Copy),
                          reads=[("ps", pu)], writes=[("Ub", u3)])

                    def ct_fn(ct):
                        u3 = ct % 3
                        b = ct % 2
                        A("dve", lambda e, ct=ct, u3=u3: e.tensor_copy(out=Ub[u3][:, 0:2], in_=tails[:, ct, :]),
                          reads=["tails"], writes=[("Ub", u3)])
                        n = 513 if last else 512
                        wk = lambda k, ct=ct: vec_t[:, V_WQK + ct * 3 + k:V_WQK + ct * 3 + k + 1]
                        A("dve", lambda e, b=b, n=n, wk=wk, u3=u3: e.tensor_scalar(cv[b][:, 0:n], Ub[u3][:, 0:n], wk(0), None, ALU.mult),
                          reads=[("Ub", u3), "vec"], writes=[("cv", b)])
                        A("dve", lambda e, b=b, n=n, wk=wk, u3=u3: e.scalar_tensor_tensor(
                            out=cv[b][:, 0:n], in0=Ub[u3][:, 1:1 + n], scalar=wk(1), in1=cv[b][:, 0:n], op0=ALU.mult, op1=ALU.add),
                          reads=[("Ub", u3), ("cv", b), "vec"], writes=[("cv", b)])
                        A("dve", lambda e, b=b, n=n, wk=wk, u3=u3: e.scalar_tensor_tensor(
                            out=cv[b][:, 0:n], in0=Ub[u3][:, 2:2 + n], scalar=wk(2), in1=cv[b][:, 0:n], op0=ALU.mult, op1=ALU.add),
                          reads=[("Ub", u3), ("cv", b), "vec"], writes=[("cv", b)])
                        cs = ct % 4
                        A("act", lambda e, b=b, n=n: e.activation(out=sgp[b][:, 0:n], in_=cv[b][:, 0:n], func=AF.Sigmoid),
                          reads=[("cv", b)], writes=[("sgp", b)])
                        A("dve", lambda e, b=b, cs=cs, n=n: e.tensor_tensor(out=cst2[cs][:, 0:n], in0=cv[b][:, 0:n], in1=sgp[b][:, 0:n], op=ALU.mult),
                          reads=[("cv", b), ("sgp", b)], writes=[("cst2", cs)])
                        dT = qbT_s if ct < 4 else kbT_s
                        r0 = (ct % 4) * 128
                        j0 = 1 if w == 0 else 0
                        A("pool", lambda e, dT=dT, r0=r0, j0=j0, n=n, a=a, cs=cs: e.dma_start(
                            out=dT[r0:r0 + 128, a - 1 + j0:a - 1 + n], in_=cst2[cs][:, j0:n]),
                          reads=[("cst2", cs)], dma=True)
                        if not last:
                            A("dve", lambda e, ct=ct, u3=u3: e.tensor_copy(out=tails[:, ct, :], in_=Ub[u3][:, 512:514]),
                              reads=[("Ub", u3)], writes=["tails"])
                    def j_fn(j):
                        def tm_mm(pi, col0, ncol, j=j, s=s):
                            for c in range(8):
                                A("pe", lambda e, pi=pi, c=c, col0=col0, ncol=ncol, j=j, s=s: e.matmul(
                                    psb[pi][:, 0:ncol], lhsT=xn[s][:, c, j * 128:(j + 1) * 128], rhs=W[:, c, col0:col0 + ncol],
                                    start=(c == 0), stop=(c == 7)),
                                  reads=[("xn", s, c), ("W", c, 0), ("W", c, 1)], writes=[("ps", pi)])
                        pv = nextps()
                        tm_mm(pv, C_VA, 144)
                        A("act", lambda e, pv=pv, j=j, s=s: e.activation(
                            out=va_st[s][:, j, :, 0:64], in_=psb[pv][:, 0:128].rearrange("p (g d) -> p g d", g=2), func=AF.Copy),
                          reads=[("ps", pv)], writes=[("va_st", s)])
                        A("dve", lambda e, pv=pv, j=j, s=s: e.tensor_tensor(out=g_st[s][:, j, :], in0=psb[pv][:, 128:144], in1=bg_t[:], op=ALU.add),
                          reads=[("ps", pv), "bg"], writes=[("g_st", s)])
                        pb = nextps()
                        tm_mm(pb, C_VB, 512)
                        A("dve", lambda e, pb=pb, j=j, s=s: e.tensor_copy(
                            out=vb_st[s][:, j, :, 0:128], in_=psb[pb][:].rearrange("p (h d) -> p h d", h=4)),
                          reads=[("ps", pb)], writes=[("vb_st", s)])
                        po = nextps()
                        tm_mm(po, C_OB, 512)
                        A("act", lambda e, po=po, j=j, s=s: e.activation(out=ob_st[s][:, j, :], in_=psb[po][:], func=AF.Sigmoid),
                          reads=[("ps", po)], writes=[("ob_st", 0)])
                    ct_a(0)
                    for i4 in range(4):
                        for ct in (2 * i4, 2 * i4 + 1):
                            if ct + 1 < 8:
                                ct_a(ct + 1)
                            ct_fn(ct)
                        j_fn(i4)
                        if i4 == 0 and w + 1 < NW:
                            p1_front_b(w + 1)
                    A("pool", lambda e, a=a, s=s: e.dma_start(
                        out=va_s[a:a + 512, :].rearrange("(j p) c -> p j c", p=128), in_=va_st[s][:].rearrange("p j g d -> p j (g d)")),
                      reads=[("va_st", s)], dma=True)
                    A("pool", lambda e, a=a, s=s: e.dma_start(
                        out=vb_s[a:a + 512, :].rearrange("(j p) c -> p j c", p=128), in_=vb_st[s][:].rearrange("p j h d -> p j (h d)")),
                      reads=[("vb_st", s)], dma=True)
                    A("pool", lambda e, a=a, s=s: e.dma_start(
                        out=ob_s[a:a + 512, :].rearrange("(j p) c -> p j c", p=128), in_=ob_st[s][:]),
                      reads=[("ob_st", 0)], dma=True)
                    A("pool", lambda e, a=a, s=s: e.dma_start(
                        out=g_s[a:a + 512, :].rearrange("(j p) c -> p j c", p=128), in_=g_st[s][:]),
                      reads=[("g_st", s)], dma=True)
                p1_front(0)
                p1_front_b(0)
                for w in range(NW):
                    p1_part1(w)
                    if w + 1 < NW:
                        p1_front(w + 1)
                    p1_part2(w)
                Sx.emit("p1")


        def weight_conv_jobs(L):
            CW = 2048
            stg = [sb(L, "wc_stg%d" % i, [128, CW]) for i in range(2)]
            stb = [sb(L, "wc_stb%d" % i, [128, CW], BF16) for i in range(2)]
            v3c = sb(L, "wc_v3", [128, 704])
            Sx.add("sp", lambda e: e.dma_start(out=v3c[:], in_=vec3[:, :]), writes=["wc_v3"], dma=True)
            jobs = []
            cnt = [0]

            def mk(src, dstd, nrow_chunks, ncols, gain_col):
                for c in range(nrow_chunks):
                    for c0 in range(0, ncols, CW):
                        n = min(CW, ncols - c0)

                        def job(src=src, dstd=dstd, c=c, c0=c0, n=n, gain_col=gain_col):
                            b = cnt[0] % 2
                            cnt[0] += 1
                            Sx.add("sp", lambda e: e.dma_start(out=stg[b][:, 0:n], in_=src[c * 128:(c + 1) * 128, c0:c0 + n]),
                                   writes=[("wc_stg", b)], dma=True)
                            eng = "dve" if b == 0 else "pool"
                            if gain_col is None:
                                Sx.add(eng, lambda e: e.tensor_copy(out=stb[b][:, 0:n], in_=stg[b][:, 0:n]),
                                       reads=[("wc_stg", b)], writes=[("wc_stb", b)])
                            else:
                                Sx.add(eng, lambda e: e.tensor_scalar(stb[b][:, 0:n], stg[b][:, 0:n], v3c[:, gain_col + c:gain_col + c + 1],
                                                                      1.0, ALU.mult, ALU.mult),
                                       reads=[("wc_stg", b), "wc_v3"], writes=[("wc_stb", b)])
                            Sx.add("pool", lambda e: e.dma_start(out=dstd[c * 128:(c + 1) * 128, c0:c0 + n], in_=stb[b][:, 0:n]),
                                   reads=[("wc_stb", b)], dma=True)
                        jobs.append(job)
            mk(w_out_d, wo_b, 8, D, None)
            mk(w_up_d[0], wu_b[0], 8, 2 * D_FF, V3_FN[0])
            mk(w_down_d[0], wd_b[0], 22, D, None)
            mk(w_pw1_d, w1_b, 8, 2 * D, V3_MO)
            mk(w_pw2_d, w2_b, 8, D, None)
            mk(w_up_d[1], wu_b[1], 8, 2 * D_FF, V3_FN[1])
            mk(w_down_d[1], wd_b[1], 22, D, None)
            return jobs

        if "pwc" in phases and "p2a" not in phases:
            with ExitStack() as L:
                for j in weight_conv_jobs(L):
                    j()
                Sx.emit("pwc")

        def load_wb(A, name, dst, srcb, ncols, order=None, PW=512):
            np_ = (ncols + PW - 1) // PW
            for p in (order if order is not None else range(np_)):
                c0 = p * PW
                n = min(PW, ncols - c0)
                A("sp", lambda e, c0=c0, n=n: e.dma_start(out=dst[:, :, c0:c0 + n], in_=srcb[:, c0:c0 + n].rearrange("(c p) n -> p c n", p=128)),
                  writes=[(name, p)], dma=True)

        if "p2a" in phases:
            with ExitStack() as L:
                A = Sx.add
                NT = S // 128
                pss = [L.enter_context(nc.psum_tensor(uq("pss%d" % i), [128, 1024], F32)) for i in range(3)]
                pso = [L.enter_context(nc.psum_tensor(uq("pso%d" % i), [128, 512], F32)) for i in range(2)]
                scnt = [0]
                sslot = {}
                KT = sb(L, "KT", [128, S], BF16)
                VE = sb(L, "VE", [128, NT, 256], BF16)
                QT = [sb(L, "QT%d" % i, [128, 4, 128], BF16) for i in range(2)]
                pt = [sb(L, "pt%d" % i, [128, 1024], BF16) for i in range(4)]
                osb = [sb(L, "osb%d" % i, [65, 512]) for i in range(2)]
                rdn = [sb(L, "rdn%d" % i, [65, 512]) for i in range(2)]
                o_st = [sb(L, "o_st%d" % i, [64, 4, 128], BF16) for i in range(4)]
                A("sp", lambda e: e.dma_start(out=cst_t[:], in_=cst[:, :]), writes=["cst"], dma=True)
                A("sp", lambda e: e.dma_start(out=KT[:], in_=kaT_s[:, :]), writes=["KT"], dma=True)
                for t0 in range(0, NT, 8):
                    t1 = min(NT, t0 + 8)
                    A("sp", lambda e, t0=t0, t1=t1: e.dma_start(out=VE[:, t0:t1, :], in_=va_s[t0 * 128:t1 * 128, :].rearrange("(t p) c -> p t c", p=128)),
                      writes=["VE"], dma=True)

                def load_q(qt):
                    qs = qt % 2
                    for g in range(2):
                        A("sp", lambda e, qt=qt, g=g, qs=qs: e.dma_start(
                            out=QT[qs][g * 64:(g + 1) * 64, :, :],
                            in_=qaT_s[g * 256:(g + 1) * 256, qt * 128:(qt + 1) * 128].rearrange("(h d) q -> d h q", d=64)),
                          writes=[("QT", qs, g)], dma=True)

                def s_mm(n):
                    qt, kc = divmod(n, NT)
                    qs = qt % 2
                    b = scnt[0] % 3
                    scnt[0] += 1
                    sslot[n] = b
                    for g in range(2):
                        A("pe", lambda e, b=b, g=g, kc=kc, qs=qs: e.matmul(
                            pss[b][:, g * 512:(g + 1) * 512], lhsT=KT[g * 64:(g + 1) * 64, kc * 128:(kc + 1) * 128],
                            rhs=QT[qs][g * 64:(g + 1) * 64, :, :].rearrange("d h q -> d (h q)"), start=True, stop=True),
                          reads=["KT", ("QT", qs, g)], writes=[("ps", "s", b)])

                def epilogue1(qt):
                    for g in range(2):
                        A("dve", lambda e, g=g: e.tensor_copy(out=osb[g][:], in_=pso[g][0:65, :]),
                          reads=[("ps", "o", g)], writes=[("osb", g)])

                def epilogue2a(qt):
                    for g in range(2):
                        A("dve", lambda e, g=g: e.reciprocal(out=rdn[g][64:65, :], in_=osb[g][64:65, :]),
                          reads=[("osb", g)], writes=[("rdn", g)])

                def epilogue2b(qt):
                    b = scnt[0] % 3
                    scnt[0] += 1
                    for g in range(2):
                        A("pe", lambda e, g=g, b=b: e.matmul(pss[b][0:64, g * 512:(g + 1) * 512], lhsT=cst_t[64:65, 128:192], rhs=rdn[g][64:65, :], start=True, stop=True),
                          reads=[("rdn", g), "cst"], writes=[("ps", "s", b)])
                    for g in range(2):
                        k = (qt * 2 + g) % 4
                        A("dve", lambda e, g=g, k=k, b=b: e.tensor_tensor(out=o_st[k][:].rearrange("d h q -> d (h q)"), in0=pss[b][0:64, g * 512:(g + 1) * 512], in1=osb[g][0:64, :], op=ALU.mult),
                          reads=[("ps", "s", b), ("osb", g)], writes=[("o_st", k)])
                        A("pool", lambda e, g=g, k=k, qt=qt: e.dma_start(
                            out=catTa_s[g * 256:(g + 1) * 256, qt * 128:(qt + 1) * 128].rearrange("(h d) q -> d h q", d=64), in_=o_st[k][:]),
                          reads=[("o_st", k)], dma=True)

                NI = NT * NT
                KB2 = min(12, NT - 2)
                wjobs = weight_conv_jobs(L)
                wstep = max(1, (NI - 8) // max(1, len(wjobs)))
                wnext = [0]
                load_q(0)
                s_mm(0)
                s_mm(1)
                pending = None
                def pv_mm(m):
                    qt, kc = divmod(m, NT)
                    p4 = m % 4
                    for g in range(2):
                        A("pe", lambda e, p4=p4, g=g, kc=kc: e.matmul(
                            pso[g][:, :], lhsT=VE[:, kc, g * 128:(g + 1) * 128], rhs=pt[p4][:, g * 512:(g + 1) * 512],
                            start=(kc == 0), stop=(kc == NT - 1)),
                          reads=[("pt", p4), "VE"], writes=[("ps", "o", g)])
                    if kc == NT - 1:
                        epilogue1(qt)

                for n in range(NI):
                    qt, kc = divmod(n, NT)
                    b = sslot[n]
                    p4 = n % 4
                    if kc == 0 and qt + 1 < NT:
                        load_q(qt + 1)
                    A("act", lambda e, b=b, p4=p4: e.activation(out=pt[p4][:], in_=pss[b][:], func=AF.Exp, scale=0.125, bias=-8.0),
                      reads=[("ps", "s", b)], writes=[("pt", p4)])
                    if kc == 2 and qt > 0:
                        epilogue2a(qt - 1)
                    if kc == KB2 and qt > 0:
                        epilogue2b(qt - 1)
                    if n % wstep == 0 and wnext[0] < len(wjobs):
                        wjobs[wnext[0]]()
                        wnext[0] += 1
                    if n + 2 < NI:
                        s_mm(n + 2)
                    if n >= 1:
                        pv_mm(n - 1)
                pv_mm(NI - 1)
                epilogue2a(NT - 1)
                epilogue2b(NT - 1)
                while wnext[0] < len(wjobs):
                    wjobs[wnext[0]]()
                    wnext[0] += 1
                Sx.emit("p2a")

        if "p2b" in phases:
            with ExitStack() as L:
                stage = [0]

                def A(*a, **k):
                    if stage[0] <= stop:
                        return Sx.add(*a, **k)
                NT = S // 128
                N4 = NT * 4
                LNC = float(np.log(128.0 ** -0.5))
                pb = [L.enter_context(nc.psum_tensor(uq("pb%d" % i), [128, 512], F32)) for i in range(7)]
                ubank = lambda h, c: (0, 1, 6)[(h * 2 + c) // 3]
                uoff = lambda h, c: ((h * 2 + c) % 3) * 129
                pkt = [L.enter_context(nc.psum_tensor(uq("pkt%d" % i), [128, 1024], BF16)) for i in range(1)] * 2
                Gt = sb(L, "Gt", [128, NT, 16])
                LF = sb(L, "LF", [128, NT, 4])
                tmpg = sb(L, "tmpg", [128, NT, 4])
                tmpg2 = sb(L, "tmpg2", [128, NT, 4])
                ebt = sb(L, "ebt", [128, NT, 4])
                ksc = sb(L, "ksc", [128, NT, 4])
                kwt = sb(L, "kwt", [128, NT, 4])
                a01 = [sb(L, "a01_%d" % i, [128, NT, 4]) for i in range(2)]
                identb = sb(L, "identb", [128, 128], BF16)
                hg_t = sb(L, "hg_t", [128, 512])
                qT = [sb(L, "qTb%d" % i, [128, 4, 128], BF16) for i in range(2)]
                kT = [sb(L, "kTb%d" % i, [128, 4, 128], BF16) for i in range(2)]
                Vt = [sb(L, "Vt%d" % i, [128, 4, 129], BF16) for i in range(2)]
                Kw = [sb(L, "Kw%d" % i, [128, 2, 4, 128], BF16) for i in range(2)]
                kw01 = [sb(L, "kw01_%d" % i, [128, NT, 4]) for i in range(2)]
                ATm = [sb(L, "ATm%d" % i, [128, 4, 128], BF16) for i in range(2)]
                C32 = sb(L, "C32", [128, 4, 129])
                Cb = [sb(L, "Cb%d" % i, [128, 4, 129], BF16) for i in range(3)]
                dd = sb(L, "dd", [128, 4])
                rr = sb(L, "rr", [128, 4])
                scl = sb(L, "scl", [128, 4])
                hout = [sb(L, "hout%d" % i, [128, 4, 128]) for i in range(2)]
                hft = [sb(L, "hft%d" % i, [128, 512]) for i in range(2)]
                obt = [sb(L, "obt%d" % i, [128, 512]) for i in range(2)]
                sqj = sb(L, "sqj", [128, 128])
                ssq = sb(L, "ssq", [128, 4])
                rsn = sb(L, "rsn", [128, 4])
                hn = sb(L, "hn", [128, 512])
                ocat = [sb(L, "ocat%d" % i, [128, 512], BF16) for i in range(2)]
                maskf = cst_t[:, 384:512]
                maskb = cst_t[:, 512:640]
                E0 = cst_t[:, 640:768]
                E1 = cst_t[:, 768:896]

                A("sp", lambda e: e.dma_start(out=cst_t[:], in_=cst[:, :]), writes=["cst"], dma=True)
                A("sp", lambda e: e.dma_start(out=hg_t[:], in_=hgt[:, :]), writes=["hg"], dma=True)
                for t0 in range(0, NT, 8):
                    t1 = min(NT, t0 + 8)
                    A("sp", lambda e, t0=t0, t1=t1: e.dma_start(out=Gt[:, t0:t1, :], in_=g_s[t0 * 128:t1 * 128, :].rearrange("(t p) c -> p t c", p=128)),
                      writes=["Gt"], dma=True)
                A("dve", lambda e: e.tensor_copy(out=identb[:], in_=cst_t[:, 0:128]), reads=["cst"], writes=["identb"])

                for dr in range(2):
                    mask = maskf if dr == 0 else maskb
                    gi = Gt[:, :, dr * 8:dr * 8 + 4]
                    gf = Gt[:, :, dr * 8 + 4:dr * 8 + 8]
                    A("act", lambda e, gf=gf: e.activation(out=LF[:], in_=gf, func=AF.Exp, scale=-1.0),
                      reads=["Gt"], writes=["LF"])
                    A("act", lambda e: e.activation(out=LF[:], in_=LF[:], func=AF.Ln, bias=1.0),
                      reads=["LF"], writes=["LF"])
                    LF2 = LF[:].rearrange("p t h -> p (t h)")
                    for bi, lh in enumerate((mask, blk64, E0, E1)):
                        A("pe", lambda e, bi=bi, lh=lh, LF2=LF2: e.matmul(pb[bi][:, 0:N4], lhsT=lh, rhs=LF2, start=True, stop=True),
                          reads=["LF", "cst"], writes=[("ps", bi)])
                    f2 = lambda t: t[:].rearrange("p t h -> p (t h)")
                    A("act", lambda e: e.activation(out=f2(ebt), in_=pb[0][:, 0:N4], func=AF.Exp, scale=-1.0),
                      reads=[("ps", 0)], writes=["ebt"])
                    A("dve", lambda e, gi=gi: e.tensor_tensor(out=tmpg[:], in0=gi, in1=pb[0][:, 0:N4].rearrange("p (t h) -> p t h", h=4), op=ALU.add),
                      reads=["Gt", ("ps", 0)], writes=["tmpg"])
                    A("act", lambda e: e.activation(out=ksc[:], in_=tmpg[:], func=AF.Exp, bias=LNC),
                      reads=["tmpg"], writes=["ksc"])
                    A("dve", lambda e: e.tensor_tensor(out=tmpg2[:], in0=tmpg[:], in1=pb[1][:, 0:N4].rearrange("p (t h) -> p t h", h=4), op=ALU.subtract),
                      reads=["tmpg", ("ps", 1)], writes=["tmpg2"])
                    A("act", lambda e: e.activation(out=kwt[:], in_=tmpg2[:], func=AF.Exp, bias=LNC),
                      reads=["tmpg2"], writes=["kwt"])
                    for c in range(2):
                        A("act", lambda e, c=c: e.activation(out=f2(a01[c]), in_=pb[2 + c][:, 0:N4], func=AF.Exp, scale=-1.0),
                          reads=[("ps", 2 + c)], writes=[("a01", c)])
                        mc = cst_t[:, 640 + 128 * c:641 + 128 * c]
                        A("dve", lambda e, c=c, mc=mc: e.tensor_scalar(kw01[c][:], kwt[:], mc, None, ALU.mult),
                          reads=["kwt", "cst"], writes=[("kw01", c)])
                    A("dve", lambda e: e.memset(C32[:], 0.0), writes=[("C32", h) for h in range(4)])
                    A("pool", lambda e: e.memset(Cb[0][:], 0.0), writes=[("Cb", 0, h) for h in range(4)])
                    gres = ["ebt", "ksc", "kwt", ("a01", 0), ("a01", 1)]
                    nstate = 0
                    fin_pending = []
                    order = list(range(NT)) if dr == 0 else list(range(NT - 1, -1, -1))
                    corder = (0, 1) if dr == 0 else (1, 0)
                    for it, t in enumerate(order):
                        s = it % 2
                        r0 = t * 128
                        stage[0] = 1
                        A("sp", lambda e, s=s, r0=r0: e.dma_start(
                            out=qT[s][:], in_=qbT_s[:, r0:r0 + 128].rearrange("(h d) q -> d h q", d=128)),
                          writes=[("qT", s)], dma=True)
                        A("sp", lambda e, s=s, r0=r0: e.dma_start(
                            out=kT[s][:], in_=kbT_s[:, r0:r0 + 128].rearrange("(h d) q -> d h q", d=128)),
                          writes=[("kT", s)], dma=True)
                        A("sp", lambda e, s=s, r0=r0: e.dma_start(
                            out=Vt[s][:], in_=vb_s[r0:r0 + 128, :].rearrange("p (h c) -> p h c", c=129)),
                          writes=[("Vt", s)], dma=True)
                        if dr == 1:
                            A("sp", lambda e, s=s, r0=r0: e.dma_start(out=hft[s][:], in_=hf_s[r0:r0 + 128, :]),
                              reads=[("hf_s", t)], writes=[("hft", s)], dma=True)
                            A("sp", lambda e, s=s, r0=r0: e.dma_start(out=obt[s][:], in_=ob_s[r0:r0 + 128, :]),
                              writes=[("obt", s)], dma=True)
                        stage[0] = 2
                        for h in range(4):
                            A("pe", lambda e, s=s, h=h: e.transpose(out=pkt[s][:, h * 128:(h + 1) * 128], in_=kT[s][:, h, :], identity=identb[:]),
                              reads=[("kT", s), "identb"], writes=[("ps", "kt", 0)])
                        for c in range(2):
                            A("dve", lambda e, s=s, t=t, c=c: e.tensor_tensor(
                                out=Kw[s][:, c, :, :], in0=pkt[s][:, 0:512].rearrange("p (h d) -> p h d", h=4),
                                in1=kw01[c][:, t, :].unsqueeze(2).to_broadcast([128, 4, 128]), op=ALU.mult),
                              reads=[("ps", "kt", 0), ("kw01", c)], writes=[("Kw", s, h) for h in range(4)])
                        while fin_pending:
                            fin_pending.pop(0)()
                        stage[0] = 3
                        for h in range(4):
                            for c in range(2):
                                A("pe", lambda e, s=s, h=h, c=c: e.matmul(
                                    pb[ubank(h, c)][:, uoff(h, c):uoff(h, c) + 129],
                                    lhsT=Kw[s][:, c, h, :], rhs=Vt[s][:, h, :],
                                    start=True, stop=True, skip_group_check=True),
                                  reads=[("Kw", s, h), ("Vt", s)], writes=[("ps", ubank(h, c))])
                        stage[0] = 4
                        pa = 2 + s
                        for h in range(4):
                            A("pe", lambda e, s=s, h=h, pa=pa: e.matmul(pb[pa][:, h * 128:(h + 1) * 128], lhsT=kT[s][:, h, :], rhs=qT[s][:, h, :],
                                                                        start=True, stop=True, skip_group_check=True),
                              reads=[("kT", s), ("qT", s)], writes=[("ps", pa)])
                        for h in range(4):
                            A("dve", lambda e, s=s, h=h, pa=pa, t=t, mask=mask: e.scalar_tensor_tensor(
                                out=ATm[s][:, h, :], in0=pb[pa][:, h * 128:(h + 1) * 128], scalar=ksc[:, t, h:h + 1], in1=mask,
                                op0=ALU.mult, op1=ALU.mult),
                              reads=[("ps", pa), "ksc", "cst"], writes=[("ATm", s, h)])
                        stage[0] = 5
                        st_in = nstate
                        for ci, c in enumerate(corder):
                            nstate += 1
                            for h in range(4):
                                A("dve", lambda e, h=h, c=c, t=t: e.scalar_tensor_tensor(
                                    out=C32[:, h, :], in0=C32[:, h, :], scalar=a01[c][:, t, h:h + 1],
                                    in1=pb[ubank(h, c)][:, uoff(h, c):uoff(h, c) + 129],
                                    op0=ALU.mult, op1=ALU.add),
                                  reads=[("C32", h), ("a01", c), ("ps", ubank(h, c))], writes=[("C32", h)])
                                A("act", lambda e, h=h, n=nstate: e.activation(out=Cb[n % 3][:, h, :], in_=C32[:, h, :], func=AF.Copy),
                                  reads=[("C32", h)], writes=[("Cb", nstate % 3, h)])
                        stage[0] = 6
                        for h in range(4):
                            pn = 4 + h // 2
                            col = (h % 2) * 129
                            A("pe", lambda e, s=s, h=h, pn=pn, col=col: e.matmul(
                                pb[pn][:, col:col + 129], lhsT=ATm[s][:, h, :], rhs=Vt[s][:, h, :],
                                start=(h % 2 == 0), stop=False, skip_group_check=True),
                              reads=[("ATm", s, h), ("Vt", s)], writes=[("ps", pn)])
                            for ci, c in enumerate(corder):
                                sn = (st_in + ci) % 3
                                A("pe", lambda e, s=s, h=h, pn=pn, col=col, c=c, sn=sn: e.matmul(
                                    pb[pn][c * 64:(c + 1) * 64, col:col + 129], lhsT=qT[s][:, h, c * 64:(c + 1) * 64], rhs=Cb[sn][:, h, :],
                                    start=False, stop=True, skip_group_check=True),
                                  reads=[("qT", s), ("Cb", sn, h)], writes=[("ps", pn)])
                        stage[0] = 7
                        for bk in range(2):
                            dv = pb[4 + bk][:, 0:258].rearrange("p (h c) -> p h c", c=129)[:, :, 128]
                            A("dve", lambda e, bk=bk, dv=dv, t=t: e.tensor_tensor(
                                out=dd[:, 2 * bk:2 * bk + 2], in0=dv, in1=ebt[:, t, 2 * bk:2 * bk + 2], op=ALU.mult),
                              reads=[("ps", 4 + bk), "ebt"], writes=[("dd", bk)])
                        A("dve", lambda e: e.scalar_tensor_tensor(out=rr[:], in0=dd[:], scalar=-1.0, in1=dd[:], op0=ALU.mult, op1=ALU.max),
                          reads=[("dd", 0), ("dd", 1)], writes=["rr"])
                        A("dve", lambda e: e.tensor_scalar(rr[:], rr[:], 1.0, None, ALU.max), reads=["rr"], writes=["rr"])
                        A("dve", lambda e: e.reciprocal(out=rr[:], in_=rr[:]), reads=["rr"], writes=["rr"])
                        A("dve", lambda e, t=t: e.tensor_tensor(out=scl[:], in0=rr[:], in1=ebt[:, t, :], op=ALU.mult),
                          reads=["rr", "ebt"], writes=["scl"])
                        for bk in range(2):
                            A("dve", lambda e, bk=bk, s=s: e.tensor_tensor(
                                out=hout[s][:, 2 * bk:2 * bk + 2, :],
                                in0=pb[4 + bk][:, 0:258].rearrange("p (h c) -> p h c", c=129)[:, :, 0:128],
                                in1=scl[:, 2 * bk:2 * bk + 2].unsqueeze(2).to_broadcast([128, 2, 128]), op=ALU.mult),
                              reads=[("ps", 4 + bk), "scl"], writes=[("hout", s, 2 * bk), ("hout", s, 2 * bk + 1)])
                        stage[0] = 8
                        hres = [("hout", s, h) for h in range(4)]
                        if dr == 0:
                            A("pool", lambda e, s=s, r0=r0: e.dma_start(out=hf_s[r0:r0 + 128, :], in_=hout[s][:].rearrange("p h d -> p (h d)")),
                              reads=hres, writes=[("hf_s", t)], dma=True)
                        else:
                            if hb_s is not None:
                                A("pool", lambda e, s=s, r0=r0: e.dma_start(out=hb_s[r0:r0 + 128, :], in_=hout[s][:].rearrange("p h d -> p (h d)")),
                                  reads=hres, dma=True)
                            A("pool", lambda e, s=s: e.tensor_tensor(out=hft[s][:], in0=hft[s][:], in1=hout[s][:].rearrange("p h d -> p (h d)"), op=ALU.add),
                              reads=hres + [("hft", s)], writes=[("hft", s)])
                            for h in range(4):
                                A("act", lambda e, s=s, h=h: e.activation(out=sqj[:], in_=hft[s][:, h * 128:(h + 1) * 128], func=AF.Square,
                                                                          accum_out=ssq[:, h:h + 1]),
                                  reads=[("hft", s)], writes=["sqj", ("ssq", h)])
                            A("act", lambda e: e.activation(out=rsn[:], in_=ssq[:], func=AF.Ln, scale=1.0 / 128, bias=EPS),
                              reads=[("ssq", h) for h in range(4)], writes=["rsn"])
                            A("act", lambda e: e.activation(out=rsn[:], in_=rsn[:], func=AF.Exp, scale=-0.5),
                              reads=["rsn"], writes=["rsn"])

                            def fin_b(s=s, r0=r0):
                                for h in range(4):
                                    A("dve", lambda e, s=s, h=h: e.scalar_tensor_tensor(
                                        out=hn[:, h * 128:(h + 1) * 128], in0=hft[s][:, h * 128:(h + 1) * 128], scalar=rsn[:, h:h + 1],
                                        in1=hg_t[:, h * 128:(h + 1) * 128], op0=ALU.mult, op1=ALU.mult),
                                      reads=[("hft", s), "rsn", "hg"], writes=[("hn", h)])
                                A("pool", lambda e, s=s: e.tensor_tensor(out=ocat[s][:], in0=hn[:], in1=obt[s][:], op=ALU.mult),
                                  reads=[("hn", h) for h in range(4)] + [("obt", s)], writes=[("ocat", s)])
                                A("pool", lambda e, s=s, r0=r0: e.dma_start(out=cat_s[r0:r0 + 128, 512:1024], in_=ocat[s][:]),
                                  reads=[("ocat", s)], dma=True)
                            fin_pending.append(fin_b)
                    while fin_pending:
                        fin_pending.pop(0)()
                Sx.emit("p2b")


        if "ptest" in phases:
            xT_in = P.inp("xT_in", [D, S])
            cat_in = P.inp("cat_in", [S, D])
            with ExitStack() as L:
                A = Sx.add
                tb = [sb(L, "tb%d" % i, [128, D]) for i in range(2)]
                tbb = [sb(L, "tbb%d" % i, [128, D], BF16) for i in range(2)]
                for c in range(8):
                    A("sp", lambda e, c=c: e.dma_start(out=xT_s[c * 128:(c + 1) * 128, :], in_=xT_in[c * 128:(c + 1) * 128, :]), dma=True)
                for t in range(S // 128):
                    b = t % 2
                    A("sp", lambda e, t=t, b=b: e.dma_start(out=tb[b][:], in_=cat_in[t * 128:(t + 1) * 128, :]), writes=[("tb", b)], dma=True)
                    A("dve", lambda e, b=b: e.tensor_copy(out=tbb[b][:], in_=tb[b][:]), reads=[("tb", b)], writes=[("tbb", b)])
                    A("sp", lambda e, t=t, b=b: e.dma_start(out=cat_s[t * 128:(t + 1) * 128, :], in_=tbb[b][:]), reads=[("tbb", b)], dma=True)
                catTa_in = P.inp("catTa_in", [512, S])
                for c in range(4):
                    b = c % 2
                    for t0 in range(0, S, 1024):
                        A("sp", lambda e, c=c, b=b, t0=t0: e.dma_start(out=tb[b][:], in_=catTa_in[c * 128:(c + 1) * 128, t0:t0 + 1024]), writes=[("tb", b)], dma=True)
                        A("dve", lambda e, b=b: e.tensor_copy(out=tbb[b][:], in_=tb[b][:]), reads=[("tb", b)], writes=[("tbb", b)])
                        A("sp", lambda e, c=c, b=b, t0=t0: e.dma_start(out=catTa_s[c * 128:(c + 1) * 128, t0:t0 + 1024], in_=tbb[b][:]), reads=[("tbb", b)], dma=True)
                Sx.emit("ptest")

        _stg = {}

        def load_w(A, L, name, dst, src, nchunk, ncols, gain_col=None, vt=None, SW=1024):
            if id(L) not in _stg:
                _stg[id(L)] = [sb(L, "wstg%d" % i, [128, SW]) for i in range(2)]
            stg = _stg[id(L)]
            name_stg = "wstg"
            k = 0
            for c in range(nchunk):
                for c0 in range(0, ncols, SW):
                    n = min(SW, ncols - c0)
                    b = k % 2
                    k += 1
                    A("sp", lambda e, b=b, c=c, c0=c0, n=n: e.dma_start(out=stg[b][:, 0:n], in_=src[c * 128:(c + 1) * 128, c0:c0 + n]),
                      writes=[(name_stg, b)], dma=True)
                    eng = "dve" if b == 0 else "pool"
                    if gain_col is None:
                        A(eng, lambda e, b=b, c=c, c0=c0, n=n: e.tensor_copy(out=dst[:, c, c0:c0 + n], in_=stg[b][:, 0:n]),
                          reads=[(name_stg, b)], writes=[(name, b)])
                    else:
                        A(eng, lambda e, b=b, c=c, c0=c0, n=n: e.tensor_scalar(dst[:, c, c0:c0 + n], stg[b][:, 0:n],
                                                                             vt[:, gain_col + c:gain_col + c + 1], 1.0, ALU.mult, ALU.mult),
                          reads=[(name_stg, b), "vec3"], writes=[(name, b)])

        def rmsnorm_fm(A, xw, xn, sqt, lnt_, rstd_, ps_stat, W):
            for c in range(8):
                b = c % 2
                A("act", lambda e, c=c, b=b: e.activation(out=sqt[b][:, 0:W], in_=xw[:, c, 0:W], func=AF.Square),
                  reads=[("xw", c)], writes=[("sqt", b)])
                A("pe", lambda e, c=c, b=b: e.matmul(ps_stat[:, 0:W], lhsT=onesb, rhs=sqt[b][:, 0:W], start=(c == 0), stop=(c == 7)),
                  reads=[("sqt", b), "cstb"], writes=[("ps", "stat")])
            A("act", lambda e: e.activation(out=lnt_[:, 0:W], in_=ps_stat[:, 0:W], func=AF.Ln, scale=1.0 / D, bias=EPS),
              reads=[("ps", "stat")], writes=["lnt"])
            A("act", lambda e: e.activation(out=rstd_[:, 0:W], in_=lnt_[:, 0:W], func=AF.Exp, scale=-0.5),
              reads=["lnt"], writes=["rstd"])
            for c in range(8):
                eng = "dve" if c % 2 == 0 else "pool"
                A(eng, lambda e, c=c: e.tensor_tensor(out=xn[:, c, 0:W], in0=xw[:, c, 0:W], in1=rstd_[:, 0:W], op=ALU.mult),
                  reads=[("xw", c), "rstd"], writes=[("xn", c)])

        def load_xwin(A, xw, src, tok0, W):
            j0 = max(0, -tok0)
            j1 = min(W, S - tok0)
            res = [("xw", c) for c in range(8)]
            if j0 > 0:
                A("pool", lambda e: e.memset(xw[:, :, 0:j0], 0.0), writes=res)
            if j1 < W:
                A("pool", lambda e: e.memset(xw[:, :, j1:W], 0.0), writes=res)
            A("sp", lambda e: e.dma_start(out=xw[:, :, j0:j1], in_=src[:, tok0 + j0:tok0 + j1].rearrange("(c p) t -> p c t", p=128)),
              writes=res, dma=True)
            return j0, j1

        if "p3a" in phases:
            with ExitStack() as L:
                A = Sx.add
                pb = [L.enter_context(nc.psum_tensor(uq("pb%d" % i), [128, 512], F32)) for i in range(4)]
                pkt = [L.enter_context(nc.psum_tensor(uq("pkt%d" % i), [128, 1024], BF16)) for i in range(2)]
                Wo = sb(L, "Wo", [128, 8, D], BF16)
                identb = sb(L, "identb", [128, 128], BF16)
                ct4 = [sb(L, "ct4_%d" % i, [128, 4, 512], BF16) for i in range(2)]
                catT = [sb(L, "catT%d" % i, [128, 8, 512], BF16) for i in range(2)]
                xw2 = [sb(L, "xw%d" % i, [128, 8, 512]) for i in range(2)]
                A("sp", lambda e: e.dma_start(out=cst_t[:], in_=cst[:, :]), writes=["cst"], dma=True)
                A("dve", lambda e: e.tensor_copy(out=identb[:], in_=cst_t[:, 0:128]), reads=["cst"], writes=["identb"])
                load_wb(A, "Wo", Wo, wo_b, D)
                for w in range(S // 512):
                    a = w * 512
                    s = w % 2
                    A("sp", lambda e, a=a, s=s: e.dma_start(out=ct4[s][:], in_=cat_s[a:a + 512, 512:1024].rearrange("(j p) d -> p j d", p=128)),
                      writes=[("ct4", s)], dma=True)
                    A("sp", lambda e, a=a, s=s: e.dma_start(out=catT[s][:, 0:4, :], in_=catTa_s[:, a:a + 512].rearrange("(c p) t -> p c t", p=128)),
                      writes=[("catT", s, c) for c in range(4)], dma=True)
                    A("sp", lambda e, a=a, s=s: e.dma_start(out=xw2[s][:], in_=xT_s[:, a:a + 512].rearrange("(c p) t -> p c t", p=128)),
                      writes=[("xw", s, c) for c in range(8)], dma=True)
                    for c in range(4, 8):
                        k = c % 2
                        for j in range(4):
                            A("pe", lambda e, k=k, j=j, c=c, s=s: e.transpose(out=pkt[k][:, j * 128:(j + 1) * 128], in_=ct4[s][:, j, (c - 4) * 128:(c - 3) * 128],
                                                                              identity=identb[:]),
                              reads=[("ct4", s), "identb"], writes=[("ps", "kt", k)])
                        if c % 2 == 0:
                            A("act", lambda e, k=k, c=c, s=s: e.activation(out=catT[s][:, c, :], in_=pkt[k][:, 0:512], func=AF.Copy),
                              reads=[("ps", "kt", k)], writes=[("catT", s, c)])
                        else:
                            A("dve", lambda e, k=k, c=c, s=s: e.tensor_copy(out=catT[s][:, c, :], in_=pkt[k][:, 0:512]),
                              reads=[("ps", "kt", k)], writes=[("catT", s, c)])
                    for oc in range(8):
                        pi = oc % 4
                        for c in range(8):
                            A("pe", lambda e, pi=pi, c=c, oc=oc, s=s: e.matmul(pb[pi][:], lhsT=Wo[:, c, oc * 128:(oc + 1) * 128], rhs=catT[s][:, c, :],
                                                                              start=(c == 0), stop=(c == 7)),
                              reads=[("catT", s, c), ("Wo", oc // 4)], writes=[("ps", pi)])
                        A("dve", lambda e, pi=pi, oc=oc, s=s: e.tensor_tensor(out=xw2[s][:, oc, :], in0=pb[pi][:], in1=xw2[s][:, oc, :], op=ALU.add),
                          reads=[("ps", pi), ("xw", s, oc)], writes=[("xw", s, oc)])
                    A("pool", lambda e, a=a, s=s: e.dma_start(out=xB_s[:, a:a + 512].rearrange("(c p) t -> p c t", p=128), in_=xw2[s][:]),
                      reads=[("xw", s, c) for c in range(8)], dma=True)
                Sx.emit("p3a")

        def ffn_phase(layer, src, dst):
            WF = 384
            OUTW = WF - 2
            NWIN = (S + OUTW - 1) // OUTW
            with ExitStack() as L:
                A = Sx.add
                pb = [L.enter_context(nc.psum_tensor(uq("pb%d" % i), [128, 512], F32)) for i in range(7)]
                v3 = sb(L, "v3", [128, 704])
                Wu = sb(L, "Wu", [128, 8, 2 * D_FF], BF16)
                Wd = sb(L, "Wd", [128, 22, D], BF16)
                xw = sb(L, "xwf", [128, 8, WF])
                xnb = [sb(L, "xnf%d" % i, [128, 8, WF], BF16) for i in range(2)]
                sqt = [sb(L, "sqt%d" % i, [128, WF], BF16) for i in range(2)]
                lnt_ = sb(L, "lntf", [128, WF])
                rstd_ = sb(L, "rstdf", [128, WF])
                hh = sb(L, "hh", [128, 22, WF], BF16)
                og = [sb(L, "og%d" % i, [128, WF]) for i in range(2)]
                ov = [sb(L, "ov%d" % i, [128, WF]) for i in range(2)]
                xres = [sb(L, "xres%d" % i, [128, WF]) for i in range(2)]
                ost = [sb(L, "ost%d" % i, [128, WF]) for i in range(2)]
                load_cst(A)
                A("sp", lambda e: e.dma_start(out=v3[:], in_=vec3[:, :]), writes=["vec3"], dma=True)
                uorder = []
                for ct in range(22):
                    for p in (ct // 4, (22 + ct) // 4):
                        if p not in uorder:
                            uorder.append(p)
                load_wb(A, "Wu", Wu, wu_b[layer], 2 * D_FF, order=uorder[:3])

                def rest_weights():
                    load_wb(A, "Wu", Wu, wu_b[layer], 2 * D_FF, order=uorder[3:])
                    for ct in range(22):
                        A("sp", lambda e, ct=ct: e.dma_start(out=Wd[:, ct, :], in_=wd_b[layer][ct * 128:(ct + 1) * 128, :]),
                          writes=[("Wd", ct)], dma=True)
                A("pool", lambda e: e.memset(hh[:], 0.0), writes=[("hh", ct) for ct in range(22)])
                wcol = V3_WFF[layer]
                bcol = V3_BFF[layer]

                def norm(wi):
                    a = wi * OUTW
                    xn = xnb[wi % 2]
                    load_xwin(A, xw, src, a - 1, WF)
                    for c in range(8):
                        b = c % 2
                        A("act", lambda e, c=c, b=b: e.activation(out=sqt[b][:], in_=xw[:, c, :], func=AF.Square),
                          reads=[("xw", c)], writes=[("sqt", b)])
                        A("pe", lambda e, c=c, b=b: e.matmul(pb[6][:, 0:WF], lhsT=onesb, rhs=sqt[b][:], start=(c == 0), stop=(c == 7)),
                          reads=[("sqt", b), "cstb"], writes=[("ps", 6)])
                    A("act", lambda e: e.activation(out=lnt_[:], in_=pb[6][:, 0:WF], func=AF.Ln, scale=1.0 / D, bias=EPS),
                      reads=[("ps", 6)], writes=["lnt"])
                    A("act", lambda e: e.activation(out=rstd_[:], in_=lnt_[:], func=AF.Exp, scale=-0.5),
                      reads=["lnt"], writes=["rstd"])
                    for c in range(8):
                        eng = "dve" if c % 2 == 0 else "pool"
                        A(eng, lambda e, c=c, xn=xn: e.tensor_tensor(out=xn[:, c, :], in0=xw[:, c, :], in1=rstd_[:], op=ALU.mult),
                          reads=[("xw", c), "rstd"], writes=[("xn", wi % 2, c)])

                def up(wi):
                    xn = xnb[wi % 2]
                    n = WF - 2
                    for ct in range(22):
                        b = ct % 2
                        for half in range(2):
                            col = half * 22 + ct
                            pi = b * 2 + half
                            for c in range(8):
                                A("pe", lambda e, pi=pi, c=c, col=col, xn=xn: e.matmul(pb[pi][:, 0:WF], lhsT=Wu[:, c, col * 128:(col + 1) * 128], rhs=xn[:, c, :],
                                                                                   start=(c == 0), stop=(c == 7)),
                                  reads=[("xn", wi % 2, c), ("Wu", col // 4)], writes=[("ps", pi)])
                            o = og[b] if half == 0 else ov[b]
                            okey = ("og", b) if half == 0 else ("ov", b)
                            wk = lambda k, col=col: v3[:, wcol + col * 3 + k:wcol + col * 3 + k + 1]
                            bk = v3[:, bcol + col:bcol + col + 1]
                            A("act", lambda e, pi=pi, o=o, wk=wk, bk=bk: e.activation(out=o[:, 0:n], in_=pb[pi][:, 0:n], func=AF.Identity, scale=wk(0), bias=bk),
                              reads=[("ps", pi), "vec3"], writes=[okey])
                            A("dve", lambda e, pi=pi, o=o, wk=wk: e.scalar_tensor_tensor(out=o[:, 0:n], in0=pb[pi][:, 1:1 + n], scalar=wk(1), in1=o[:, 0:n],
                                                                                   op0=ALU.mult, op1=ALU.add),
                              reads=[("ps", pi), okey, "vec3"], writes=[okey])
                            A("dve", lambda e, pi=pi, o=o, wk=wk: e.scalar_tensor_tensor(out=o[:, 0:n], in0=pb[pi][:, 2:2 + n], scalar=wk(2), in1=o[:, 0:n],
                                                                                   op0=ALU.mult, op1=ALU.add),
                              reads=[("ps", pi), okey, "vec3"], writes=[okey])
                        A("act", lambda e, b=b: e.activation(out=og[b][:, 0:n], in_=og[b][:, 0:n], func=AF.Silu),
                          reads=[("og", b)], writes=[("og", b)])
                        A("pool", lambda e, b=b, ct=ct: e.tensor_tensor(out=hh[:, ct, 1:1 + n], in0=og[b][:, 0:n], in1=ov[b][:, 0:n], op=ALU.mult),
                          reads=[("og", b), ("ov", b)], writes=[("hh", ct)])

                def down(wi):
                    a = wi * OUTW
                    nout = min(OUTW, S - a)
                    for oc in range(8):
                        pi = 4 + oc % 2
                        r = oc % 2
                        A("sp", lambda e, oc=oc, r=r, a=a, nout=nout: e.dma_start(out=xres[r][:, 1:1 + nout], in_=src[oc * 128:(oc + 1) * 128, a:a + nout]),
                          writes=[("xres", r)], dma=True)
                        for ct in range(22):
                            A("pe", lambda e, pi=pi, ct=ct, oc=oc: e.matmul(pb[pi][:, 0:WF], lhsT=Wd[:, ct, oc * 128:(oc + 1) * 128], rhs=hh[:, ct, :],
                                                                          start=(ct == 0), stop=(ct == 21)),
                              reads=[("hh", ct), ("Wd", ct)], writes=[("ps", pi)])
                        A("dve", lambda e, pi=pi, r=r, nout=nout: e.tensor_tensor(out=ost[r][:, 1:1 + nout], in0=pb[pi][:, 1:1 + nout], in1=xres[r][:, 1:1 + nout], op=ALU.add),
                          reads=[("ps", pi), ("xres", r)], writes=[("ost", r)])
                        A("pool", lambda e, oc=oc, r=r, a=a, nout=nout: e.dma_start(out=dst[oc * 128:(oc + 1) * 128, a:a + nout], in_=ost[r][:, 1:1 + nout]),
                          reads=[("ost", r)], dma=True)

                norm(0)
                rest_weights()
                for wi in range(NWIN):
                    up(wi)
                    if wi + 1 < NWIN:
                        norm(wi + 1)
                    down(wi)
                Sx.emit("ffn%d" % layer)

        if "p3b" in phases:
            ffn_phase(0, xB_s, xT_s)

        if "p3c" in phases:
            WC = 512
            OUTW = WC - 30
            NWIN = (S + OUTW - 1) // OUTW
            with ExitStack() as L:
                A = Sx.add
                pb = [L.enter_context(nc.psum_tensor(uq("pb%d" % i), [128, 512], F32)) for i in range(8)]
                v3 = sb(L, "v3", [128, 704])
                W1 = sb(L, "W1c", [128, 8, 2 * D], BF16)
                W2 = sb(L, "W2c", [128, 8, D], BF16)
                identb = sb(L, "identb", [128, 128], BF16)
                dg = sb(L, "dg", [128, 8, 31, 128], BF16)
                xw = sb(L, "xwc", [128, 8, WC])
                xn = sb(L, "xnc", [128, 8, WC], BF16)
                sqt = [sb(L, "sqt%d" % i, [128, WC], BF16) for i in range(2)]
                lnt_ = sb(L, "lntc", [128, WC])
                rstd_ = sb(L, "rstdc", [128, WC])
                sg = [sb(L, "sg%d" % i, [128, WC]) for i in range(2)]
                glu = sb(L, "glu", [128, 8, WC], BF16)
                cvt = sb(L, "cvt", [128, 8, OUTW])
                acc = [sb(L, "cacc%d" % i, [128, OUTW]) for i in range(2)]
                sgm = [sb(L, "sgm%d" % i, [128, OUTW]) for i in range(2)]
                mean = sb(L, "mean", [128, OUTW])
                msq = sb(L, "msq", [128, OUTW])
                sn = sb(L, "snc", [128, 8, OUTW], BF16)
                load_cst(A)
                A("sp", lambda e: e.dma_start(out=v3[:], in_=vec3[:, :]), writes=["vec3"], dma=True)
                A("dve", lambda e: e.tensor_copy(out=identb[:], in_=cst_t[:, 0:128]), reads=["cst"], writes=["identb"])
                load_wb(A, "W1c", W1, w1_b, 2 * D, order=[0, 2, 1, 3])
                load_wb(A, "W2c", W2, w2_b, D)
                for ct in range(8):
                    for k in range(31):
                        eng = "dve" if (k % 2 == 0) else "pool"
                        A(eng, lambda e, ct=ct, k=k: e.tensor_scalar(dg[:, ct, k, :], identb[:], v3[:, V3_WDC + ct * 31 + k:V3_WDC + ct * 31 + k + 1],
                                                                   1.0, ALU.mult, ALU.mult),
                          reads=["identb", "vec3"], writes=[("dg", ct, k % 2)])
                pctr = [0]

                def nps():
                    pctr[0] += 1
                    return pctr[0] % 7
                xres = [sb(L, "xresc%d" % i, [128, OUTW]) for i in range(2)]
                ost = [sb(L, "ostc%d" % i, [128, OUTW]) for i in range(2)]
                jj = {}

                def c_load(wi):
                    jj[wi] = load_xwin(A, xw, xT_s, wi * OUTW - 15, WC)

                def c_norm(wi):
                    rmsnorm_fm(A, xw, xn, sqt, lnt_, rstd_, pb[7], WC)

                def c_pw1(wi, ct):
                    j0, j1 = jj[wi]
                    b = ct % 2
                    pa = nps()
                    for c in range(8):
                        A("pe", lambda e, pa=pa, c=c, ct=ct: e.matmul(pb[pa][:], lhsT=W1[:, c, ct * 128:(ct + 1) * 128], rhs=xn[:, c, :],
                                                                    start=(c == 0), stop=(c == 7)),
                          reads=[("xn", c), ("W1c", ct // 4)], writes=[("ps", pa)])
                    pg = nps()
                    for c in range(8):
                        A("pe", lambda e, pg=pg, c=c, ct=ct: e.matmul(pb[pg][:], lhsT=W1[:, c, D + ct * 128:D + (ct + 1) * 128], rhs=xn[:, c, :],
                                                                    start=(c == 0), stop=(c == 7)),
                          reads=[("xn", c), ("W1c", 2 + ct // 4)], writes=[("ps", pg)])
                    A("act", lambda e, pg=pg, b=b, ct=ct: e.activation(out=sg[b][:], in_=pb[pg][:], func=AF.Sigmoid,
                                                                     bias=v3[:, V3_BPW1 + 8 + ct:V3_BPW1 + 9 + ct]),
                      reads=[("ps", pg), "vec3"], writes=[("sg", b)])
                    A("dve", lambda e, pa=pa, b=b, ct=ct: e.scalar_tensor_tensor(out=glu[:, ct, :], in0=pb[pa][:], scalar=v3[:, V3_BPW1 + ct:V3_BPW1 + ct + 1],
                                                                               in1=sg[b][:], op0=ALU.add, op1=ALU.mult),
                      reads=[("ps", pa), ("sg", b), "vec3"], writes=[("glu", ct)])
                    if j0 > 0:
                        A("pool", lambda e, ct=ct, j0=j0: e.memset(glu[:, ct, 0:j0], 0.0), writes=[("glu", ct)])
                    if j1 < WC:
                        A("pool", lambda e, ct=ct, j1=j1: e.memset(glu[:, ct, j1:WC], 0.0), writes=[("glu", ct)])

                NDV = 6

                def c_conv(wi, ct):
                    pc = nps()
                    ab = ct % 2
                    wk = lambda k: v3[:, V3_WDC + ct * 31 + k:V3_WDC + ct * 31 + k + 1]
                    for k in range(NDV, 31):
                        A("pe", lambda e, pc=pc, ct=ct, k=k: e.matmul(pb[pc][:, 0:OUTW], lhsT=dg[:, ct, k, :], rhs=glu[:, ct, k:k + OUTW],
                                                                    start=(k == NDV), stop=(k == 30)),
                          reads=[("glu", ct), ("dg", ct, k % 2)], writes=[("ps", pc)])
                    A("dve", lambda e, ct=ct, ab=ab, wk=wk: e.tensor_scalar(acc[ab][:], glu[:, ct, 0:OUTW], wk(0), v3[:, V3_BDC + ct:V3_BDC + ct + 1],
                                                                         ALU.mult, ALU.add),
                      reads=[("glu", ct), "vec3"], writes=[("acc", ab)])
                    for k in range(1, NDV):
                        A("dve", lambda e, ct=ct, ab=ab, k=k, wk=wk: e.scalar_tensor_tensor(out=acc[ab][:], in0=glu[:, ct, k:k + OUTW], scalar=wk(k), in1=acc[ab][:],
                                                                                      op0=ALU.mult, op1=ALU.add),
                          reads=[("glu", ct), ("acc", ab), "vec3"], writes=[("acc", ab)])
                    A("dve", lambda e, pc=pc, ct=ct, ab=ab: e.tensor_tensor(out=cvt[:, ct, :], in0=pb[pc][:, 0:OUTW], in1=acc[ab][:], op=ALU.add),
                      reads=[("ps", pc), ("acc", ab)], writes=[("cvt", ct)])

                def c_lnstats(wi):
                    p1 = nps()
                    for ct in range(8):
                        A("pe", lambda e, p1=p1, ct=ct: e.matmul(pb[p1][:, 0:OUTW], lhsT=ones, rhs=cvt[:, ct, :], start=(ct == 0), stop=(ct == 7)),
                          reads=[("cvt", ct)], writes=[("ps", p1)])
                    p2 = nps()
                    for ct in range(8):
                        b = ct % 2
                        A("act", lambda e, ct=ct, b=b: e.activation(out=sqt[b][:, 0:OUTW], in_=cvt[:, ct, :], func=AF.Square),
                          reads=[("cvt", ct)], writes=[("sqt", b)])
                        A("pe", lambda e, p2=p2, ct=ct, b=b: e.matmul(pb[p2][:, 0:OUTW], lhsT=onesb, rhs=sqt[b][:, 0:OUTW], start=(ct == 0), stop=(ct == 7)),
                          reads=[("sqt", b), "cstb"], writes=[("ps", p2)])
                    A("act", lambda e, p1=p1: e.activation(out=mean[:], in_=pb[p1][:, 0:OUTW], func=AF.Copy, scale=1.0 / D),
                      reads=[("ps", p1)], writes=["mean"])
                    A("act", lambda e: e.activation(out=msq[:], in_=mean[:], func=AF.Square), reads=["mean"], writes=["msq"])
                    A("dve", lambda e, p2=p2: e.scalar_tensor_tensor(out=msq[:], in0=pb[p2][:, 0:OUTW], scalar=1.0 / D, in1=msq[:],
                                                                    op0=ALU.mult, op1=ALU.subtract),
                      reads=[("ps", p2), "msq"], writes=["msq"])
                    A("act", lambda e: e.activation(out=msq[:], in_=msq[:], func=AF.Ln, bias=EPS), reads=["msq"], writes=["msq"])
                    A("act", lambda e: e.activation(out=msq[:], in_=msq[:], func=AF.Exp, scale=-0.5), reads=["msq"], writes=["msq"])

                def c_normalize(wi, ct):
                    eng = "dve" if ct % 2 == 0 else "pool"
                    sb_ = ct % 2
                    gcol = v3[:, V3_LNG + ct:V3_LNG + ct + 1]
                    bcol = v3[:, V3_LNB + ct:V3_LNB + ct + 1]
                    A(eng, lambda e, ct=ct: e.tensor_tensor(out=cvt[:, ct, :], in0=cvt[:, ct, :], in1=mean[:], op=ALU.subtract),
                      reads=[("cvt", ct), "mean"], writes=[("cvt", ct)])
                    A(eng, lambda e, ct=ct: e.tensor_tensor(out=cvt[:, ct, :], in0=cvt[:, ct, :], in1=msq[:], op=ALU.mult),
                      reads=[("cvt", ct), "msq"], writes=[("cvt", ct)])
                    A("act", lambda e, ct=ct, sb_=sb_, gcol=gcol, bcol=bcol: e.activation(out=sgm[sb_][:], in_=cvt[:, ct, :], func=AF.Sigmoid, scale=gcol, bias=bcol),
                      reads=[("cvt", ct), "vec3"], writes=[("sgm", sb_)])
                    A(eng, lambda e, ct=ct, gcol=gcol, bcol=bcol: e.tensor_scalar(cvt[:, ct, :], cvt[:, ct, :], gcol, bcol, ALU.mult, ALU.add),
                      reads=[("cvt", ct), "vec3", ("sgm", sb_)], writes=[("cvt", ct)])
                    A(eng, lambda e, ct=ct, sb_=sb_: e.tensor_tensor(out=sn[:, ct, :], in0=cvt[:, ct, :], in1=sgm[sb_][:], op=ALU.mult),
                      reads=[("cvt", ct), ("sgm", sb_)], writes=[("sn", ct)])

                def c_pw2(wi):
                    a = wi * OUTW
                    nout = min(OUTW, S - a)
                    for oc in range(8):
                        r = oc % 2
                        A("sp", lambda e, oc=oc, r=r, a=a, nout=nout: e.dma_start(out=xres[r][:, 0:nout], in_=xT_s[oc * 128:(oc + 1) * 128, a:a + nout]),
                          writes=[("xres", r)], dma=True)
                        po = nps()
                        for c in range(8):
                            A("pe", lambda e, po=po, c=c, oc=oc: e.matmul(pb[po][:, 0:OUTW], lhsT=W2[:, c, oc * 128:(oc + 1) * 128], rhs=sn[:, c, :],
                                                                        start=(c == 0), stop=(c == 7)),
                              reads=[("sn", c), ("W2c", oc // 4)], writes=[("ps", po)])
                        A("dve", lambda e, po=po, oc=oc, r=r, nout=nout: e.scalar_tensor_tensor(
                            out=ost[r][:, 0:nout], in0=pb[po][:, 0:nout], scalar=v3[:, V3_BPW2 + oc:V3_BPW2 + oc + 1], in1=xres[r][:, 0:nout],
                            op0=ALU.add, op1=ALU.add),
                          reads=[("ps", po), ("xres", r), "vec3"], writes=[("ost", r)])
                        A("pool", lambda e, oc=oc, r=r, a=a, nout=nout: e.dma_start(out=xB_s[oc * 128:(oc + 1) * 128, a:a + nout], in_=ost[r][:, 0:nout]),
                          reads=[("ost", r)], dma=True)

                c_load(0)
                c_norm(0)
                for ct in range(8):
                    c_pw1(0, ct)
                for wi in range(NWIN):
                    more = wi + 1 < NWIN
                    if more:
                        c_load(wi + 1)
                    for ct in range(4):
                        c_conv(wi, ct)
                    if more:
                        c_norm(wi + 1)
                    for ct in range(4, 8):
                        c_conv(wi, ct)
                    c_lnstats(wi)
                    for ct in range(8):
                        if more:
                            c_pw1(wi + 1, ct)
                        c_normalize(wi, ct)
                    c_pw2(wi)
                Sx.emit("p3c")

        if "p3d" in phases:
            ffn_phase(1, xB_s, xT_s)

        if "p3e" in phases:
            with ExitStack() as L:
                A = Sx.add
                pb = [L.enter_context(nc.psum_tensor(uq("pb%d" % i), [128, 512], F32)) for i in range(4)]
                xw2 = [sb(L, "xwe%d" % i, [128, 8, 512]) for i in range(2)]
                yt = [sb(L, "yt%d" % i, [128, D]) for i in range(3)]
                A("sp", lambda e: e.dma_start(out=cst_t[:], in_=cst[:, :]), writes=["cst"], dma=True)
                k = 0
                for w in range(S // 512):
                    a = w * 512
                    s = w % 2
                    A("sp", lambda e, a=a, s=s: e.dma_start(out=xw2[s][:], in_=xT_s[:, a:a + 512].rearrange("(c p) t -> p c t", p=128)),
                      writes=[("xw", s)], dma=True)
                    for j in range(4):
                        yb = k % 3
                        for hf in range(2):
                            pi = (k * 2 + hf) % 4
                            for c4 in range(4):
                                c = hf * 4 + c4
                                A("pe", lambda e, pi=pi, c=c, c4=c4, j=j, s=s: e.transpose(out=pb[pi][:, c4 * 128:(c4 + 1) * 128],
                                                                                          in_=xw2[s][:, c, j * 128:(j + 1) * 128], identity=ident),
                                  reads=[("xw", s), "cst"], writes=[("ps", pi)])
                            if hf == 0:
                                A("act", lambda e, pi=pi, yb=yb: e.activation(out=yt[yb][:, 0:512], in_=pb[pi][:], func=AF.Copy),
                                  reads=[("ps", pi)], writes=[("yt", yb, 0)])
                            else:
                                A("dve", lambda e, pi=pi, yb=yb: e.tensor_copy(out=yt[yb][:, 512:1024], in_=pb[pi][:]),
                                  reads=[("ps", pi)], writes=[("yt", yb, 1)])
                        A("pool", lambda e, a=a, j=j, yb=yb: e.dma_start(out=y_d[a + j * 128:a + (j + 1) * 128, :], in_=yt[yb][:]),
                          reads=[("yt", yb, 0), ("yt", yb, 1)], dma=True)
                        k += 1
                Sx.emit("p3e")
    return P


def rope_tables(S):
    t = np.arange(S)
    row = (t // GRID_W).astype(np.float32)
    col = (t % GRID_W).astype(np.float32)
    inv = (np.float32(10000.0) ** (-np.arange(16, dtype=np.float32) / np.float32(16))).astype(np.float32)
    C = np.zeros((128, S), np.float32)
    Sg = np.zeros((128, S), np.float32)
    for p in range(128):
        d = p % 64
        sec, half, pair = d // 32, (d % 32) // 16, d % 16
        ang = (row if sec == 0 else col) * inv[pair]
        C[p] = np.cos(ang)
        Sg[p] = np.sin(ang) * (-1.0 if half == 0 else 1.0)
    return C, Sg


def rot_perm64():
    idx = np.arange(64)
    sec, half, pair = idx // 32, (idx % 32) // 16, idx % 16
    return sec * 32 + (1 - half) * 16 + pair


def host_prep(S, inputs):
    f = lambda a: np.ascontiguousarray(np.asarray(a, dtype=np.float32))
    w_in = f(inputs["w_in"][0])
    perm = rot_perm64()
    qa = w_in[:, 0:512]
    ka = w_in[:, 512:640]
    va = w_in[:, 640:768]
    qb = w_in[:, 768:1280]
    kb = w_in[:, 1280:1792]
    vb = w_in[:, 1792:2304]
    ob = w_in[:, 2304:2816]
    gt = w_in[:, 2816:2832]
    qar = qa.reshape(D, 8, 64)[:, :, perm].reshape(D, 512)
    kar = ka.reshape(D, 2, 64)[:, :, perm].reshape(D, 128)
    w_in_ext = np.concatenate([qa, qar, ka, kar, qb, kb, va, gt, vb, ob], axis=1)
    assert w_in_ext.shape[1] == NCOL_IN
    cst = np.zeros((128, 1024), np.float32)
    cst[:, 0:128] = np.eye(128, dtype=np.float32)
    cst[:, 128:256] = 1.0
    cst[0:64, 256:320] = 1.0
    cst[64:128, 320:384] = 1.0
    kk = np.arange(128)[:, None]
    ll = np.arange(128)[None, :]
    same = (kk // 64) == (ll // 64)
    cst[:, 384:512] = (same & (kk <= ll)).astype(np.float32)
    cst[:, 512:640] = (same & (kk >= ll)).astype(np.float32)
    cst[0:64, 640:768] = 1.0
    cst[64:128, 768:896] = 1.0
    C, Sg = rope_tables(S)
    vecs = np.zeros((128, 64), np.float32)
    vecs[:, 0:8] = f(inputs["mix_norm_e"][0]).reshape(8, 128).T
    gq = f(inputs["q_gain_a"][0])
    gk = f(inputs["k_gain_a"][0])
    vecs[:, 8] = np.tile(gq, 2)
    vecs[:, 9] = np.tile(gq[perm], 2)
    vecs[:, 10] = np.tile(gk, 2)
    vecs[:, 11] = np.tile(gk[perm], 2)
    wqk = f(inputs["w_qk_conv_b"][0])
    for ct in range(8):
        for k in range(3):
            vecs[:, 12 + ct * 3 + k] = wqk[k, ct * 128:(ct + 1) * 128]
    bgt = np.tile(f(inputs["b_gates_b"][0])[None, :], (128, 1))
    hgt = np.tile(f(inputs["h_gain_b"][0])[None, :], (128, 1))
    v3 = np.zeros((128, 704), np.float32)
    pc = lambda v: f(v).reshape(-1, 128).T
    v3[:, 0:8] = pc(inputs["ffn_norm"][0])
    v3[:, 8:16] = pc(inputs["ffn_norm"][1])
    v3[:, 16:24] = pc(inputs["mix_norm_o"][0])
    v3[:, 24:40] = pc(inputs["b_pw1_c"][0])
    wdc = f(inputs["w_dw_c"][0])
    for ct in range(8):
        v3[:, 40 + ct * 31:40 + (ct + 1) * 31] = wdc[:, ct * 128:(ct + 1) * 128].T
    v3[:, 288:296] = pc(inputs["b_dw_c"][0])
    v3[:, 296:304] = pc(inputs["ln_g_c"][0])
    v3[:, 304:312] = pc(inputs["ln_b_c"][0])
    v3[:, 312:320] = pc(inputs["b_pw2_c"][0])
    for l in range(2):
        wff = f(inputs["w_dw_ff"][l])
        base = (320, 452)[l]
        for ct in range(44):
            v3[:, base + ct * 3:base + ct * 3 + 3] = wff[:, ct * 128:(ct + 1) * 128].T
        v3[:, (584, 628)[l]:(584, 628)[l] + 44] = pc(inputs["b_dw_ff"][l])
    extra = dict(w_out=f(inputs["w_out_e"][0]), w_up0=f(inputs["w_up"][0]), w_up1=f(inputs["w_up"][1]),
                 w_down0=f(inputs["w_down"][0]), w_down1=f(inputs["w_down"][1]), w_pw1=f(inputs["w_pw1_c"][0]),
                 w_pw2=f(inputs["w_pw2_c"][0]), vec3=v3)
    return dict(w_in_ext=np.ascontiguousarray(w_in_ext), cst=cst, ropeC=C, ropeS=Sg, vecs=vecs, bgt=bgt, hgt=hgt, **extra)


_CACHE = {}


def kernel(**inputs):
    S = 8192
    shared = host_prep(S, inputs)
    xs = [np.asarray(inputs["x_prompt"][i], np.float32) for i in range(4)] + \
         [np.asarray(inputs["x_sample"][i], np.float32) for i in range(2)]
    xs = xs + [np.zeros_like(xs[0]), np.zeros_like(xs[0])]
    P = build(S, phases=("p1", "p2a", "p2b", "p3a", "p3b", "p3c", "p3d", "p3e"))
    in_maps = []
    for c in range(8):
        m = dict(shared)
        m["x"] = np.ascontiguousarray(xs[c])
        in_maps.append(m)
    res = run_bass_kernel_spmd(P.nc, in_maps, core_ids=list(range(8)))
    ys = [np.asarray(res.results[c]["y"], np.float32) for c in range(6)]
    return (np.stack(ys[0:4], 0), np.stack(ys[4:6], 0))
```

```python
import numpy as np
from contextlib import ExitStack
import concourse.bass as bass
import concourse.mybir as mybir
from concourse.bass_utils import run_bass_kernel_spmd

F32 = mybir.dt.float32
BF16 = mybir.dt.bfloat16
AF = mybir.ActivationFunctionType
ALU = mybir.AluOpType
AX = mybir.AxisListType

D = 1024
GRID_W = 64
EPS = 1e-6
D_FF = 2816
NCOL_IN = 3472
C_QA, C_QAR, C_KA, C_KAR, C_QB, C_KB, C_VA, C_G, C_VB, C_OB = 0, 512, 1024, 1152, 1280, 1792, 2304, 2432, 2448, 2960


class Op:
    __slots__ = ("eng", "fn", "deps", "sig", "ev", "dma", "dslot", "dprev")


class Sched:
    ENGS = ("pe", "act", "dve", "pool", "sp")
    EPOCH = 16000

    def __init__(self, nc, stack, n_dma=24, same_engine_sync=True):
        self.nc = nc
        self.stack = stack
        self.esem = {e: stack.enter_context(nc.semaphore("se_%s_0" % e)) for e in self.ENGS}
        self.eepoch = {e: 0 for e in self.ENGS}
        self.ecnt = {e: 0 for e in self.ENGS}
        self.dsem = [stack.enter_context(nc.semaphore("sd_%d" % i)) for i in range(n_dma)]
        self.dcnt = [0] * n_dma
        self.dnext = 0
        self.same = same_engine_sync
        self.embed = True
        self.waited = {e: {} for e in self.ENGS}
        self.reset()

    def reset(self):
        self.ops = []
        self.lastw = {}
        self.readers = {}

    def add(self, eng, fn, reads=(), writes=(), dma=False):
        op = Op()
        op.eng, op.fn, op.dma, op.sig, op.ev = eng, fn, dma, False, None
        excl = [r for r in reads if isinstance(r, tuple) and r[0] == "ps"]
        if excl:
            reads = [r for r in reads if r not in excl]
            writes = list(writes) + excl
        deps = set()
        for r in reads:
            w = self.lastw.get(r)
            if w is not None:
                deps.add(w)
        for r in writes:
            w = self.lastw.get(r)
            if w is not None:
                deps.add(w)
            for q in self.readers.get(r, ()):
                deps.add(q)
        deps.discard(op)
        op.deps = deps
        for r in reads:
            self.readers.setdefault(r, []).append(op)
        for r in writes:
            self.lastw[r] = op
            self.readers[r] = []
        if dma:
            i = self.dnext
            self.dnext = (self.dnext + 1) % len(self.dsem)
            op.dslot = i
            op.dprev = self.dcnt[i]
            self.dcnt[i] += 16
            op.ev = (self.dsem[i], self.dcnt[i])
        self.ops.append(op)
        return op

    def _needs_wait(self, op, d):
        if d.dma:
            return True
        if d.eng != op.eng:
            return True
        if op.eng == "pe" or op.dma:
            return False
        return self.same

    def emit(self, name):
        nc = self.nc
        ops = self.ops
        for op in ops:
            for d in op.deps:
                if not d.dma and self._needs_wait(op, d):
                    d.sig = True
        for op in ops:
            if op.dma or not op.sig:
                continue
            e = op.eng
            if self.ecnt[e] >= self.EPOCH:
                self.eepoch[e] += 1
                self.esem[e] = self.stack.enter_context(nc.semaphore("se_%s_%d" % (e, self.eepoch[e])))
                self.ecnt[e] = 0
            self.ecnt[e] += 1
            op.ev = (self.esem[e], self.ecnt[e])
        per = {e: [o for o in ops if o.eng == e] for e in self.ENGS}
        dsem_final = [(self.dsem[i], self.dcnt[i]) for i in range(len(self.dsem)) if self.dcnt[i] > 0]
        sched = self

        def body(ename):
            def f(eng):
                waited = sched.waited[ename]

                def wait(sem, val):
                    k = id(sem)
                    if waited.get(k, 0) >= val:
                        return
                    waited[k] = val
                    eng.wait_ge(sem, val)

                for op in per[ename]:
                    need = {}
                    for d in op.deps:
                        if sched._needs_wait(op, d):
                            s, v = d.ev
                            if need.get(id(s), (None, 0))[1] < v:
                                need[id(s)] = (s, v)
                    if op.dma and op.dprev > 0:
                        s = sched.dsem[op.dslot]
                        if need.get(id(s), (None, 0))[1] < op.dprev:
                            need[id(s)] = (s, op.dprev)
                    todo = [(s, v) for s, v in need.values() if waited.get(id(s), 0) < v]
                    emb = None
                    if todo and sched.embed:
                        emb = todo.pop()
                    for s, v in todo:
                        wait(s, v)
                    ins = op.fn(eng)
                    if emb is not None:
                        waited[id(emb[0])] = emb[1]
                        ins._wait_ge(emb[0], emb[1])
                    if op.dma:
                        ins.then_inc(sched.dsem[op.dslot], 16)
                    elif op.sig:
                        ins.then_inc(op.ev[0], 1)
                if ename == "sp":
                    for s, v in dsem_final:
                        wait(s, v)
            return f

        with nc.Block() as block:
            block.sync(body("sp"))
            block.tensor(body("pe"))
            block.scalar(body("act"))
            block.vector(body("dve"))
            block.gpsimd(body("pool"))
        self.reset()


class Prog:
    def __init__(self, S=8192, debug=()):
        self.S = S
        self.debug = set(debug)
        self.nc = bass.Bass("TRN2", target_bir_lowering=False)
        self.din = {}
        self.dout = {}

    def inp(self, name, shape, dt=F32):
        t = self.nc.dram_tensor(name, list(shape), dt, kind="ExternalInput").ap()
        self.din[name] = t
        return t

    def scratch(self, name, shape, dt):
        kind = "ExternalOutput" if name in self.debug else "Internal"
        t = self.nc.dram_tensor(name, list(shape), dt, kind=kind).ap()
        if name in self.debug:
            self.dout[name] = t
        return t

    def out(self, name, shape, dt=F32):
        t = self.nc.dram_tensor(name, list(shape), dt, kind="ExternalOutput").ap()
        self.dout[name] = t
        return t


def build(S=8192, debug=(), phases=("p1",), stop=99):
    P = Prog(S, debug)
    nc = P.nc
    NW = S // 512
    x = P.inp("x", [S, D])
    w_in = P.inp("w_in_ext", [D, NCOL_IN])
    cst = P.inp("cst", [128, 1024])
    ropeC = P.inp("ropeC", [128, S])
    ropeS = P.inp("ropeS", [128, S])
    vecs = P.inp("vecs", [128, 64])
    bgt = P.inp("bgt", [128, 16])
    V_GE, V_GQ, V_GQR, V_GK, V_GKR, V_WQK = 0, 8, 9, 10, 11, 12

    xT_s = P.scratch("xT_s", [D, S], F32)
    qaT_s = P.scratch("qaT_s", [512, S], BF16)
    kaT_s = P.scratch("kaT_s", [128, S], BF16)
    va_s = P.scratch("va_s", [S, 2 * 128], BF16)
    qbT_s = P.scratch("qbT_s", [512, S], BF16)
    kbT_s = P.scratch("kbT_s", [512, S], BF16)
    vb_s = P.scratch("vb_s", [S, 4 * 129], BF16)
    ob_s = P.scratch("ob_s", [S, 512], F32)
    g_s = P.scratch("g_s", [S, 16], F32)
    cat_s = P.scratch("cat_s", [S, 1024], BF16)
    catTa_s = P.scratch("catTa_s", [512, S], BF16)
    hf_s = P.scratch("hf_s", [S, 512], F32)
    hgt = P.inp("hgt", [128, 512])
    hb_s = P.scratch("hb_s", [S, 512], F32) if "hb_s" in debug else None
    xB_s = P.scratch("xB_s", [D, S], F32)
    w_out_d = P.inp("w_out", [D, D])
    w_up_d = [P.inp("w_up%d" % i, [D, 2 * D_FF]) for i in range(2)]
    w_down_d = [P.inp("w_down%d" % i, [D_FF, D]) for i in range(2)]
    w_pw1_d = P.inp("w_pw1", [D, 2 * D])
    w_pw2_d = P.inp("w_pw2", [D, D])
    vec3 = P.inp("vec3", [128, 704])
    y_d = P.out("y", [S, D])
    wo_b = P.scratch("wo_b", [D, D], BF16)
    wu_b = [P.scratch("wu_b%d" % i, [D, 2 * D_FF], BF16) for i in range(2)]
    wd_b = [P.scratch("wd_b%d" % i, [D_FF, D], BF16) for i in range(2)]
    w1_b = P.scratch("w1_b", [D, 2 * D], BF16)
    w2_b = P.scratch("w2_b", [D, D], BF16)
    V3_FN = (0, 8); V3_MO = 16; V3_BPW1 = 24; V3_WDC = 40; V3_BDC = 288; V3_LNG = 296; V3_LNB = 304; V3_BPW2 = 312
    V3_WFF = (320, 452); V3_BFF = (584, 628)

    with ExitStack() as G:
        _uq = [0]

        def uq(name):
            _uq[0] += 1
            return "%s_u%d" % (name, _uq[0])
        sb = lambda st, name, shape, dt=F32: st.enter_context(nc.sbuf_tensor(uq(name), list(shape), dt))
        Sx = Sched(nc, G)
        pctr = [0]

        def nextps():
            i = pctr[0] % 8
            pctr[0] += 1
            return i

        cst_t = sb(G, "cst_t", [128, 1024])
        vec_t = sb(G, "vec_t", [128, 64])
        bg_t = sb(G, "bg_t", [128, 16])
        cstb = sb(G, "cstb", [128, 384], BF16)
        ident = cst_t[:, 0:128]
        ones = cst_t[:, 128:256]
        blk64 = cst_t[:, 256:384]
        onesb = cstb[:, 128:256]
        blk64b = cstb[:, 256:384]

        def load_cst(A):
            A("sp", lambda e: e.dma_start(out=cst_t[:], in_=cst[:, :]), writes=["cst"], dma=True)
            A("dve", lambda e: e.tensor_copy(out=cstb[:], in_=cst_t[:, 0:384]), reads=["cst"], writes=["cstb"])

        if "p1" in phases:
            with ExitStack() as L:
                stage = [0]
                psb = [L.enter_context(nc.psum_tensor(uq("psb%d" % i), [128, 512], F32)) for i in range(8)]

                def A(*a, **k):
                    if stage[0] <= stop:
                        return Sx.add(*a, **k)
                W = sb(L, "W1", [128, 8, NCOL_IN], BF16)
                wst = [sb(L, "wst%d" % i, [128, NCOL_IN // 2]) for i in range(2)]
                xtok = [sb(L, "xtok%d" % i, [128, 4, D]) for i in range(1)] * 2
                xT = [sb(L, "xT%d" % i, [128, 8, 512]) for i in range(1)] * 2
                sq = sb(L, "sq", [128, 8, 512], BF16)
                rstd = sb(L, "rstd", [128, 512])
                lnt = sb(L, "lnt", [128, 512])
                xn = [sb(L, "xn%d" % i, [128, 8, 512], BF16) for i in range(2)]
                Ct = [sb(L, "Ct%d" % i, [128, 512]) for i in range(1)] * 2
                St = [sb(L, "St%d" % i, [128, 512]) for i in range(1)] * 2
                sqq = [sb(L, "sqq%d" % i, [128, 512], BF16) for i in range(2)]
                rq = [sb(L, "rq%d" % i, [128, 512]) for i in range(1)] * 2
                t1 = [sb(L, "t1_%d" % i, [128, 512]) for i in range(2)]
                t2 = [sb(L, "t2_%d" % i, [128, 512]) for i in range(2)]
                qst = [sb(L, "qst%d" % i, [128, 512], BF16) for i in range(4)]
                Ub = [sb(L, "Ub%d" % i, [128, 516]) for i in range(2)]
                tails = sb(L, "tails", [128, 8, 2])
                cv = [sb(L, "cv%d" % i, [128, 516]) for i in range(2)]
                sgp = [sb(L, "sgp%d" % i, [128, 516]) for i in range(2)]
                cst2 = [sb(L, "cst2_%d" % i, [128, 516], BF16) for i in range(4)]
                va_st = [sb(L, "va_st%d" % i, [128, 4, 2, 128], BF16) for i in range(2)]
                vb_st = [sb(L, "vb_st%d" % i, [128, 4, 4, 129], BF16) for i in range(2)]
                ob_st = [sb(L, "ob_st%d" % i, [128, 4, 512]) for i in range(1)] * 2
                g_st = [sb(L, "g_st%d" % i, [128, 4, 16]) for i in range(2)]

                load_cst(A)
                A("sp", lambda e: e.dma_start(out=vec_t[:], in_=vecs[:, :]), writes=["vec"], dma=True)
                A("sp", lambda e: e.dma_start(out=bg_t[:], in_=bgt[:, :]), writes=["bg"], dma=True)
                half = NCOL_IN // 2
                for c in range(8):
                    for h in range(2):
                        A("sp", lambda e, c=c, h=h: e.dma_start(out=wst[h][:], in_=w_in[c * 128:(c + 1) * 128, h * half:(h + 1) * half]),
                          writes=[("wst", h)], dma=True)
                    A("dve", lambda e, c=c: e.tensor_scalar(W[:, c, 0:half], wst[0][:],
                                                            vec_t[:, V_GE + c:V_GE + c + 1], None, ALU.mult),
                      reads=[("wst", 0), "vec"], writes=[("W", c, 0)])
                    A("pool", lambda e, c=c: e.tensor_scalar(W[:, c, half:], wst[1][:],
                                                             vec_t[:, V_GE + c:V_GE + c + 1], 1.0, ALU.mult, ALU.mult),
                      reads=[("wst", 1), "vec"], writes=[("W", c, 1)])
                Wres = [("W", c, h) for c in range(8) for h in range(2)]
                for s in range(2):
                    A("pool", lambda e, s=s: e.memset(va_st[s][:], 0.0), writes=[("va_st", s)])
                    A("pool", lambda e, s=s: e.memset(va_st[s][:, :, :, 64:65], 1.0), writes=[("va_st", s)])
                    A("pool", lambda e, s=s: e.memset(vb_st[s][:], 1.0), writes=[("vb_st", s)])
                for b in range(2):
                    A("pool", lambda e, b=b: e.memset(Ub[b][:], 0.0), writes=[("Ub", b)])
                A("pool", lambda e: e.memset(tails[:], 0.0), writes=["tails"])

                def p1_front(w):
                    a = w * 512
                    s = w % 2
                    last = (w == NW - 1)
                    stage[0] = 1
                    A("sp", lambda e, a=a, s=s: e.dma_start(
                        out=xtok[s][:], in_=x[a:a + 512, :].rearrange("(j p) d -> p j d", p=128)),
                      writes=[("xtok", 0)], dma=True)
                    A("sp", lambda e, a=a, s=s: e.dma_start(out=Ct[s][:], in_=ropeC[:, a:a + 512]),
                      writes=[("Ct", 0)], dma=True)
                    A("sp", lambda e, a=a, s=s: e.dma_start(out=St[s][:], in_=ropeS[:, a:a + 512]),
                      writes=[("St", 0)], dma=True)
                    for c in range(8):
                        pi = nextps()
                        for j in range(4):
                            A("pe", lambda e, pi=pi, j=j, c=c, s=s: e.transpose(
                                out=psb[pi][:, j * 128:(j + 1) * 128], in_=xtok[s][:, j, c * 128:(c + 1) * 128],
                                identity=ident), reads=[("xtok", 0), "cst"], writes=[("ps", pi)])
                        if c % 2 == 0:
                            A("act", lambda e, pi=pi, c=c, s=s: e.activation(out=xT[s][:, c, :], in_=psb[pi][:], func=AF.Copy),
                              reads=[("ps", pi)], writes=[("xT", 0, c)])
                        else:
                            A("dve", lambda e, pi=pi, c=c, s=s: e.tensor_copy(out=xT[s][:, c, :], in_=psb[pi][:]),
                              reads=[("ps", pi)], writes=[("xT", 0, c)])
                    xTres = [("xT", 0, c) for c in range(8)]
                    A("pool", lambda e, a=a, s=s: e.dma_start(
                        out=xT_s[:, a:a + 512].rearrange("(c p) t -> p c t", p=128), in_=xT[s][:]),
                      reads=xTres, dma=True)
                    stage[0] = 2
                    A("act", lambda e, s=s: e.activation(out=sq[:], in_=xT[s][:], func=AF.Square),
                      reads=xTres, writes=["sq"])
                def p1_front_b(w):
                    a = w * 512
                    s = w % 2
                    last = (w == NW - 1)
                    pi = nextps()
                    for c in range(8):
                        A("pe", lambda e, pi=pi, c=c: e.matmul(psb[pi][:], lhsT=onesb, rhs=sq[:, c, :],
                                                               start=(c == 0), stop=(c == 7)),
                          reads=["sq", "cstb"], writes=[("ps", pi)])
                    A("act", lambda e, pi=pi: e.activation(out=lnt[:], in_=psb[pi][:], func=AF.Ln, scale=1.0 / D, bias=EPS),
                      reads=[("ps", pi)], writes=["lnt"])
                    A("act", lambda e: e.activation(out=rstd[:], in_=lnt[:], func=AF.Exp, scale=-0.5),
                      reads=["lnt"], writes=["rstd"])
                    for c in range(8):
                        eng = "dve" if c % 2 == 0 else "pool"
                        A(eng, lambda e, c=c, s=s: e.tensor_tensor(out=xn[s][:, c, :], in0=xT[s][:, c, :], in1=rstd[:], op=ALU.mult),
                          reads=[("xT", 0, c), "rstd"], writes=[("xn", s, c)])
                    xnres = [("xn", s, c) for c in range(8)]

                def p1_part1(w):
                    a = w * 512
                    s = w % 2
                    last = (w == NW - 1)
                    def fm_mm(pi, col0, s=s):
                        for c in range(8):
                            A("pe", lambda e, pi=pi, c=c, col0=col0, s=s: e.matmul(
                                psb[pi][:], lhsT=W[:, c, col0:col0 + 128], rhs=xn[s][:, c, :],
                                start=(c == 0), stop=(c == 7)),
                              reads=[("xn", s, c), ("W", c, 0), ("W", c, 1)], writes=[("ps", pi)])

                    stage[0] = 3
                    hst = {}

                    def hp_a(hp):
                        colq = (C_QA + hp * 128) if hp < 4 else C_KA
                        colr = (C_QAR + hp * 128) if hp < 4 else C_KAR
                        b = hp % 2
                        pq = nextps()
                        fm_mm(pq, colq)
                        pr = nextps()
                        fm_mm(pr, colr)
                        A("act", lambda e, pq=pq, b=b: e.activation(out=sqq[b][:], in_=psb[pq][:], func=AF.Square),
                          reads=[("ps", pq)], writes=[("sqq", b)])
                        hst[hp] = (pq, pr)

                    def hp_b(hp):
                        pq, pr = hst[hp]
                        vg = V_GQ if hp < 4 else V_GK
                        vgr = V_GQR if hp < 4 else V_GKR
                        b = hp % 2
                        ph = nextps()
                        A("pe", lambda e, ph=ph, b=b: e.matmul(psb[ph][:], lhsT=blk64b, rhs=sqq[b][:], start=True, stop=True),
                          reads=[("sqq", b), "cstb"], writes=[("ps", ph)])
                        A("act", lambda e, ph=ph, b=b: e.activation(out=rq[b][:], in_=psb[ph][:], func=AF.Ln, scale=1.0 / 64, bias=EPS),
                          reads=[("ps", ph)], writes=[("rq", 0)])
                        A("act", lambda e, b=b: e.activation(out=rq[b][:], in_=rq[b][:], func=AF.Exp, scale=-0.5),
                          reads=[("rq", 0)], writes=[("rq", 0)])
                        stage[0] = 3.2
                        A("dve", lambda e, pq=pq, b=b, vg=vg, s=s: e.scalar_tensor_tensor(
                            out=t1[b][:], in0=psb[pq][:], scalar=vec_t[:, vg:vg + 1], in1=Ct[s][:], op0=ALU.mult, op1=ALU.mult),
                          reads=[("ps", pq), ("Ct", 0), "vec"], writes=[("t1", b)])
                        A("dve", lambda e, pr=pr, b=b, vgr=vgr, s=s: e.scalar_tensor_tensor(
                            out=t2[b][:], in0=psb[pr][:], scalar=vec_t[:, vgr:vgr + 1], in1=St[s][:], op0=ALU.mult, op1=ALU.mult),
                          reads=[("ps", pr), ("St", 0), "vec"], writes=[("t2", b)])
                        stage[0] = 3.3
                        A("pool", lambda e, b=b: e.tensor_tensor(out=t1[b][:], in0=t1[b][:], in1=t2[b][:], op=ALU.add),
                          reads=[("t1", b), ("t2", b)], writes=[("t1", b)])
                        qs = hp % 4
                        A("pool", lambda e, b=b, qs=qs: e.tensor_tensor(out=qst[qs][:], in0=t1[b][:], in1=rq[b][:], op=ALU.mult),
                          reads=[("t1", b), ("rq", 0)], writes=[("qst", qs)])
                        stage[0] = 3.4
                        dst = qaT_s[hp * 128:(hp + 1) * 128, a:a + 512] if hp < 4 else kaT_s[:, a:a + 512]
                        A("pool", lambda e, dst=dst, qs=qs: e.dma_start(out=dst, in_=qst[qs][:]),
                          reads=[("qst", qs)], dma=True)

                    hp_a(0)
                    for hp in range(5):
                        if hp + 1 < 5:
                            hp_a(hp + 1)
                        hp_b(hp)

                def p1_part2(w):
                    a = w * 512
                    s = w % 2
                    last = (w == NW - 1)
                    def fm_mm(pi, col0, s=s):
                        for c in range(8):
                            A("pe", lambda e, pi=pi, c=c, col0=col0, s=s: e.matmul(
                                psb[pi][:], lhsT=W[:, c, col0:col0 + 128], rhs=xn[s][:, c, :],
                                start=(c == 0), stop=(c == 7)),
                              reads=[("xn", s, c), ("W", c, 0), ("W", c, 1)], writes=[("ps", pi)])

                    stage[0] = 4
                    def ct_fn(ct):
                        col = C_QB + ct * 128
                        pu = nextps()
                        fm_mm(pu, col)
                        b = ct % 2
                        A("act", lambda e, pu=pu, b=b: e.activation(out=Ub[b][:, 2:514], in_=psb[pu][:], func=AF.Copy),
                          reads=[("ps", pu)], writes=[("Ub", b)])
                        A("pool", lambda e, ct=ct, b=b: e.tensor_copy(out=Ub[b][:, 0:2], in_=tails[:, ct, :]),
                          reads=["tails"], writes=[("Ub", b)])
                        n = 513 if last else 512
                        wk = lambda k, ct=ct: vec_t[:, V_WQK + ct * 3 + k:V_WQK + ct * 3 + k + 1]
                        A("dve", lambda e, b=b, n=n, wk=wk: e.tensor_scalar(cv[b][:, 0:n], Ub[b][:, 0:n], wk(0), None, ALU.mult),
                          reads=[("Ub", b), "vec"], writes=[("cv", b)])
                        A("dve", lambda e, b=b, n=n, wk=wk: e.scalar_tensor_tensor(
                            out=cv[b][:, 0:n], in0=Ub[b][:, 1:1 + n], scalar=wk(1), in1=cv[b][:, 0:n], op0=ALU.mult, op1=ALU.add),
                          reads=[("Ub", b), ("cv", b), "vec"], writes=[("cv", b)])
                        A("dve", lambda e, b=b, n=n, wk=wk: e.scalar_tensor_tensor(
                            out=cv[b][:, 0:n], in0=Ub[b][:, 2:2 + n], scalar=wk(2), in1=cv[b][:, 0:n], op0=ALU.mult, op1=ALU.add),
                          reads=[("Ub", b), ("cv", b), "vec"], writes=[("cv", b)])
                        cs = ct % 4
                        A("act", lambda e, b=b, n=n: e.activation(out=sgp[b][:, 0:n], in_=cv[b][:, 0:n], func=AF.Sigmoid),
                          reads=[("cv", b)], writes=[("sgp", b)])
                        A("dve", lambda e, b=b, cs=cs, n=n: e.tensor_tensor(out=cst2[cs][:, 0:n], in0=cv[b][:, 0:n], in1=sgp[b][:, 0:n], op=ALU.mult),
                          reads=[("cv", b), ("sgp", b)], writes=[("cst2", cs)])
                        dT = qbT_s if ct < 4 else kbT_s
                        r0 = (ct % 4) * 128
                        j0 = 1 if w == 0 else 0
                        A("pool", lambda e, dT=dT, r0=r0, j0=j0, n=n, a=a, cs=cs: e.dma_start(
                            out=dT[r0:r0 + 128, a - 1 + j0:a - 1 + n], in_=cst2[cs][:, j0:n]),
                          reads=[("cst2", cs)], dma=True)
                        if not last:
                            A("pool", lambda e, ct=ct, b=b: e.tensor_copy(out=tails[:, ct, :], in_=Ub[b][:, 512:514]),
                              reads=[("Ub", b)], writes=["tails"])
                    def j_fn(j):
                        def tm_mm(pi, col0, ncol, j=j, s=s):
                            for c in range(8):
                                A("pe", lambda e, pi=pi, c=c, col0=col0, ncol=ncol, j=j, s=s: e.matmul(
                                    psb[pi][:, 0:ncol], lhsT=xn[s][:, c, j * 128:(j + 1) * 128], rhs=W[:, c, col0:col0 + ncol],
                                    start=(c == 0), stop=(c == 7)),
                                  reads=[("xn", s, c), ("W", c, 0), ("W", c, 1)], writes=[("ps", pi)])
                        pv = nextps()
                        tm_mm(pv, C_VA, 144)
                        A("act", lambda e, pv=pv, j=j, s=s: e.activation(
                            out=va_st[s][:, j, :, 0:64], in_=psb[pv][:, 0:128].rearrange("p (g d) -> p g d", g=2), func=AF.Copy),
                          reads=[("ps", pv)], writes=[("va_st", s)])
                        A("dve", lambda e, pv=pv, j=j, s=s: e.tensor_tensor(out=g_st[s][:, j, :], in0=psb[pv][:, 128:144], in1=bg_t[:], op=ALU.add),
                          reads=[("ps", pv), "bg"], writes=[("g_st", s)])
                        pb = nextps()
                        tm_mm(pb, C_VB, 512)
                        A("dve", lambda e, pb=pb, j=j, s=s: e.tensor_copy(
                            out=vb_st[s][:, j, :, 0:128], in_=psb[pb][:].rearrange("p (h d) -> p h d", h=4)),
                          reads=[("ps", pb)], writes=[("vb_st", s)])
                        po = nextps()
                        tm_mm(po, C_OB, 512)
                        A("act", lambda e, po=po, j=j, s=s: e.activation(out=ob_st[s][:, j, :], in_=psb[po][:], func=AF.Sigmoid),
                          reads=[("ps", po)], writes=[("ob_st", 0)])
                    for i4 in range(4):
                        ct_fn(2 * i4)
                        ct_fn(2 * i4 + 1)
                        j_fn(i4)
                        if i4 == 0 and w + 1 < NW:
                            p1_front_b(w + 1)
                    A("pool", lambda e, a=a, s=s: e.dma_start(
                        out=va_s[a:a + 512, :].rearrange("(j p) c -> p j c", p=128), in_=va_st[s][:].rearrange("p j g d -> p j (g d)")),
                      reads=[("va_st", s)], dma=True)
                    A("pool", lambda e, a=a, s=s: e.dma_start(
                        out=vb_s[a:a + 512, :].rearrange("(j p) c -> p j c", p=128), in_=vb_st[s][:].rearrange("p j h d -> p j (h d)")),
                      reads=[("vb_st", s)], dma=True)
                    A("pool", lambda e, a=a, s=s: e.dma_start(
                        out=ob_s[a:a + 512, :].rearrange("(j p) c -> p j c", p=128), in_=ob_st[s][:]),
                      reads=[("ob_st", 0)], dma=True)
                    A("pool", lambda e, a=a, s=s: e.dma_start(
                        out=g_s[a:a + 512, :].rearrange("(j p) c -> p j c", p=128), in_=g_st[s][:]),
                      reads=[("g_st", s)], dma=True)
                p1_front(0)
                p1_front_b(0)
                for w in range(NW):
                    p1_part1(w)
                    if w + 1 < NW:
                        p1_front(w + 1)
                    p1_part2(w)
                Sx.emit("p1")


        def weight_conv_jobs(L):
            CW = 2048
            stg = [sb(L, "wc_stg%d" % i, [128, CW]) for i in range(2)]
            stb = [sb(L, "wc_stb%d" % i, [128, CW], BF16) for i in range(2)]
            v3c = sb(L, "wc_v3", [128, 704])
            Sx.add("sp", lambda e: e.dma_start(out=v3c[:], in_=vec3[:, :]), writes=["wc_v3"], dma=True)
            jobs = []
            cnt = [0]

            def mk(src, dstd, nrow_chunks, ncols, gain_col):
                for c in range(nrow_chunks):
                    for c0 in range(0, ncols, CW):
                        n = min(CW, ncols - c0)

                        def job(src=src, dstd=dstd, c=c, c0=c0, n=n, gain_col=gain_col):
                            b = cnt[0] % 2
                            cnt[0] += 1
                            Sx.add("sp", lambda e: e.dma_start(out=stg[b][:, 0:n], in_=src[c * 128:(c + 1) * 128, c0:c0 + n]),
                                   writes=[("wc_stg", b)], dma=True)
                            eng = "dve" if b == 0 else "pool"
                            if gain_col is None:
                                Sx.add(eng, lambda e: e.tensor_copy(out=stb[b][:, 0:n], in_=stg[b][:, 0:n]),
                                       reads=[("wc_stg", b)], writes=[("wc_stb", b)])
                            else:
                                Sx.add(eng, lambda e: e.tensor_scalar(stb[b][:, 0:n], stg[b][:, 0:n], v3c[:, gain_col + c:gain_col + c + 1],
                                                                      1.0, ALU.mult, ALU.mult),
                                       reads=[("wc_stg", b), "wc_v3"], writes=[("wc_stb", b)])
                            Sx.add("pool", lambda e: e.dma_start(out=dstd[c * 128:(c + 1) * 128, c0:c0 + n], in_=stb[b][:, 0:n]),
                                   reads=[("wc_stb", b)], dma=True)
                        jobs.append(job)
            mk(w_out_d, wo_b, 8, D, None)
            mk(w_up_d[0], wu_b[0], 8, 2 * D_FF, V3_FN[0])
            mk(w_down_d[0], wd_b[0], 22, D, None)
            mk(w_pw1_d, w1_b, 8, 2 * D, V3_MO)
            mk(w_pw2_d, w2_b, 8, D, None)
            mk(w_up_d[1], wu_b[1], 8, 2 * D_FF, V3_FN[1])
            mk(w_down_d[1], wd_b[1], 22, D, None)
            return jobs

        if "pwc" in phases and "p2a" not in phases:
            with ExitStack() as L:
                for j in weight_conv_jobs(L):
                    j()
                Sx.emit("pwc")

        def load_wb(A, name, dst, srcb, ncols, order=None, PW=512):
            np_ = (ncols + PW - 1) // PW
            for p in (order if order is not None else range(np_)):
                c0 = p * PW
                n = min(PW, ncols - c0)
                A("sp", lambda e, c0=c0, n=n: e.dma_start(out=dst[:, :, c0:c0 + n], in_=srcb[:, c0:c0 + n].rearrange("(c p) n -> p c n", p=128)),
                  writes=[(name, p)], dma=True)

        if "p2a" in phases:
            with ExitStack() as L:
                A = Sx.add
                NT = S // 128
                pss = [L.enter_context(nc.psum_tensor(uq("pss%d" % i), [128, 1024], F32)) for i in range(3)]
                pso = [L.enter_context(nc.psum_tensor(uq("pso%d" % i), [128, 512], F32)) for i in range(2)]
                scnt = [0]
                sslot = {}
                KT = sb(L, "KT", [128, S], BF16)
                VE = sb(L, "VE", [128, NT, 256], BF16)
                QT = [sb(L, "QT%d" % i, [128, 4, 128], BF16) for i in range(2)]
                pt = [sb(L, "pt%d" % i, [128, 1024], BF16) for i in range(4)]
                osb = [sb(L, "osb%d" % i, [65, 512]) for i in range(2)]
                rdn = [sb(L, "rdn%d" % i, [65, 512]) for i in range(2)]
                o_st = [sb(L, "o_st%d" % i, [64, 4, 128], BF16) for i in range(4)]
                A("sp", lambda e: e.dma_start(out=cst_t[:], in_=cst[:, :]), writes=["cst"], dma=True)
                A("sp", lambda e: e.dma_start(out=KT[:], in_=kaT_s[:, :]), writes=["KT"], dma=True)
                for t0 in range(0, NT, 8):
                    t1 = min(NT, t0 + 8)
                    A("sp", lambda e, t0=t0, t1=t1: e.dma_start(out=VE[:, t0:t1, :], in_=va_s[t0 * 128:t1 * 128, :].rearrange("(t p) c -> p t c", p=128)),
                      writes=["VE"], dma=True)

                def load_q(qt):
                    qs = qt % 2
                    for g in range(2):
                        A("sp", lambda e, qt=qt, g=g, qs=qs: e.dma_start(
                            out=QT[qs][g * 64:(g + 1) * 64, :, :],
                            in_=qaT_s[g * 256:(g + 1) * 256, qt * 128:(qt + 1) * 128].rearrange("(h d) q -> d h q", d=64)),
                          writes=[("QT", qs, g)], dma=True)

                def s_mm(n):
                    qt, kc = divmod(n, NT)
                    qs = qt % 2
                    b = scnt[0] % 3
                    scnt[0] += 1
                    sslot[n] = b
                    for g in range(2):
                        A("pe", lambda e, b=b, g=g, kc=kc, qs=qs: e.matmul(
                            pss[b][:, g * 512:(g + 1) * 512], lhsT=KT[g * 64:(g + 1) * 64, kc * 128:(kc + 1) * 128],
                            rhs=QT[qs][g * 64:(g + 1) * 64, :, :].rearrange("d h q -> d (h q)"), start=True, stop=True),
                          reads=["KT", ("QT", qs, g)], writes=[("ps", "s", b)])

                def epilogue1(qt):
                    for g in range(2):
                        A("dve", lambda e, g=g: e.tensor_copy(out=osb[g][:], in_=pso[g][0:65, :]),
                          reads=[("ps", "o", g)], writes=[("osb", g)])

                def epilogue2a(qt):
                    for g in range(2):
                        A("dve", lambda e, g=g: e.reciprocal(out=rdn[g][64:65, :], in_=osb[g][64:65, :]),
                          reads=[("osb", g)], writes=[("rdn", g)])

                def epilogue2b(qt):
                    b = scnt[0] % 3
                    scnt[0] += 1
                    for g in range(2):
                        A("pe", lambda e, g=g, b=b: e.matmul(pss[b][0:64, g * 512:(g + 1) * 512], lhsT=cst_t[64:65, 128:192], rhs=rdn[g][64:65, :], start=True, stop=True),
                          reads=[("rdn", g), "cst"], writes=[("ps", "s", b)])
                    for g in range(2):
                        k = (qt * 2 + g) % 4
                        A("dve", lambda e, g=g, k=k, b=b: e.tensor_tensor(out=o_st[k][:].rearrange("d h q -> d (h q)"), in0=pss[b][0:64, g * 512:(g + 1) * 512], in1=osb[g][0:64, :], op=ALU.mult),
                          reads=[("ps", "s", b), ("osb", g)], writes=[("o_st", k)])
                        A("pool", lambda e, g=g, k=k, qt=qt: e.dma_start(
                            out=catTa_s[g * 256:(g + 1) * 256, qt * 128:(qt + 1) * 128].rearrange("(h d) q -> d h q", d=64), in_=o_st[k][:]),
                          reads=[("o_st", k)], dma=True)

                NI = NT * NT
                KB2 = min(12, NT - 2)
                wjobs = weight_conv_jobs(L)
                wstep = max(1, (NI - 8) // max(1, len(wjobs)))
                wnext = [0]
                load_q(0)
                s_mm(0)
                s_mm(1)
                pending = None
                def pv_mm(m):
                    qt, kc = divmod(m, NT)
                    p4 = m % 4
                    for g in range(2):
                        A("pe", lambda e, p4=p4, g=g, kc=kc: e.matmul(
                            pso[g][:, :], lhsT=VE[:, kc, g * 128:(g + 1) * 128], rhs=pt[p4][:, g * 512:(g + 1) * 512],
                            start=(kc == 0), stop=(kc == NT - 1)),
                          reads=[("pt", p4), "VE"], writes=[("ps", "o", g)])
                    if kc == NT - 1:
                        epilogue1(qt)

                for n in range(NI):
                    qt, kc = divmod(n, NT)
                    b = sslot[n]
                    p4 = n % 4
                    if kc == 0 and qt + 1 < NT:
                        load_q(qt + 1)
                    A("act", lambda e, b=b, p4=p4: e.activation(out=pt[p4][:], in_=pss[b][:], func=AF.Exp, scale=0.125, bias=-8.0),
                      reads=[("ps", "s", b)], writes=[("pt", p4)])
                    if kc == 2 and qt > 0:
                        epilogue2a(qt - 1)
                    if kc == KB2 and qt > 0:
                        epilogue2b(qt - 1)
                    if n % wstep == 0 and wnext[0] < len(wjobs):
                        wjobs[wnext[0]]()
                        wnext[0] += 1
                    if n + 2 < NI:
                        s_mm(n + 2)
                    if n >= 1:
                        pv_mm(n - 1)
                pv_mm(NI - 1)
                epilogue2a(NT - 1)
                epilogue2b(NT - 1)
                while wnext[0] < len(wjobs):
                    wjobs[wnext[0]]()
                    wnext[0] += 1
                Sx.emit("p2a")

        if "p2b" in phases:
            with ExitStack() as L:
                stage = [0]

                def A(*a, **k):
                    if stage[0] <= stop:
                        return Sx.add(*a, **k)
                NT = S // 128
                N4 = NT * 4
                LNC = float(np.log(128.0 ** -0.5))
                pb = [L.enter_context(nc.psum_tensor(uq("pb%d" % i), [128, 512], F32)) for i in range(7)]
                ubank = lambda h, c: (0, 1, 6)[(h * 2 + c) // 3]
                uoff = lambda h, c: ((h * 2 + c) % 3) * 129
                pkt = [L.enter_context(nc.psum_tensor(uq("pkt%d" % i), [128, 1024], BF16)) for i in range(1)] * 2
                Gt = sb(L, "Gt", [128, NT, 16])
                LF = sb(L, "LF", [128, NT, 4])
                tmpg = sb(L, "tmpg", [128, NT, 4])
                tmpg2 = sb(L, "tmpg2", [128, NT, 4])
                ebt = sb(L, "ebt", [128, NT, 4])
                ksc = sb(L, "ksc", [128, NT, 4])
                kwt = sb(L, "kwt", [128, NT, 4])
                a01 = [sb(L, "a01_%d" % i, [128, NT, 4]) for i in range(2)]
                identb = sb(L, "identb", [128, 128], BF16)
                hg_t = sb(L, "hg_t", [128, 512])
                qT = [sb(L, "qTb%d" % i, [128, 4, 128], BF16) for i in range(2)]
                kT = [sb(L, "kTb%d" % i, [128, 4, 128], BF16) for i in range(2)]
                Vt = [sb(L, "Vt%d" % i, [128, 4, 129], BF16) for i in range(2)]
                Kw = [sb(L, "Kw%d" % i, [128, 2, 4, 128], BF16) for i in range(2)]
                kw01 = [sb(L, "kw01_%d" % i, [128, NT, 4]) for i in range(2)]
                ATm = [sb(L, "ATm%d" % i, [128, 4, 128], BF16) for i in range(2)]
                C32 = sb(L, "C32", [128, 4, 129])
                Cb = [sb(L, "Cb%d" % i, [128, 4, 129], BF16) for i in range(3)]
                dd = sb(L, "dd", [128, 4])
                rr = sb(L, "rr", [128, 4])
                scl = sb(L, "scl", [128, 4])
                hout = [sb(L, "hout%d" % i, [128, 4, 128]) for i in range(2)]
                hft = [sb(L, "hft%d" % i, [128, 512]) for i in range(2)]
                obt = [sb(L, "obt%d" % i, [128, 512]) for i in range(2)]
                sqj = sb(L, "sqj", [128, 128])
                ssq = sb(L, "ssq", [128, 4])
                rsn = sb(L, "rsn", [128, 4])
                hn = sb(L, "hn", [128, 512])
                ocat = [sb(L, "ocat%d" % i, [128, 512], BF16) for i in range(2)]
                maskf = cst_t[:, 384:512]
                maskb = cst_t[:, 512:640]
                E0 = cst_t[:, 640:768]
                E1 = cst_t[:, 768:896]

                A("sp", lambda e: e.dma_start(out=cst_t[:], in_=cst[:, :]), writes=["cst"], dma=True)
                A("sp", lambda e: e.dma_start(out=hg_t[:], in_=hgt[:, :]), writes=["hg"], dma=True)
                for t0 in range(0, NT, 8):
                    t1 = min(NT, t0 + 8)
                    A("sp", lambda e, t0=t0, t1=t1: e.dma_start(out=Gt[:, t0:t1, :], in_=g_s[t0 * 128:t1 * 128, :].rearrange("(t p) c -> p t c", p=128)),
                      writes=["Gt"], dma=True)
                A("dve", lambda e: e.tensor_copy(out=identb[:], in_=cst_t[:, 0:128]), reads=["cst"], writes=["identb"])

                for dr in range(2):
                    mask = maskf if dr == 0 else maskb
                    gi = Gt[:, :, dr * 8:dr * 8 + 4]
                    gf = Gt[:, :, dr * 8 + 4:dr * 8 + 8]
                    A("act", lambda e, gf=gf: e.activation(out=LF[:], in_=gf, func=AF.Exp, scale=-1.0),
                      reads=["Gt"], writes=["LF"])
                    A("act", lambda e: e.activation(out=LF[:], in_=LF[:], func=AF.Ln, bias=1.0),
                      reads=["LF"], writes=["LF"])
                    LF2 = LF[:].rearrange("p t h -> p (t h)")
                    for bi, lh in enumerate((mask, blk64, E0, E1)):
                        A("pe", lambda e, bi=bi, lh=lh, LF2=LF2: e.matmul(pb[bi][:, 0:N4], lhsT=lh, rhs=LF2, start=True, stop=True),
                          reads=["LF", "cst"], writes=[("ps", bi)])
                    f2 = lambda t: t[:].rearrange("p t h -> p (t h)")
                    A("act", lambda e: e.activation(out=f2(ebt), in_=pb[0][:, 0:N4], func=AF.Exp, scale=-1.0),
                      reads=[("ps", 0)], writes=["ebt"])
                    A("dve", lambda e, gi=gi: e.tensor_tensor(out=tmpg[:], in0=gi, in1=pb[0][:, 0:N4].rearrange("p (t h) -> p t h", h=4), op=ALU.add),
                      reads=["Gt", ("ps", 0)], writes=["tmpg"])
                    A("act", lambda e: e.activation(out=ksc[:], in_=tmpg[:], func=AF.Exp, bias=LNC),
                      reads=["tmpg"], writes=["ksc"])
                    A("dve", lambda e: e.tensor_tensor(out=tmpg2[:], in0=tmpg[:], in1=pb[1][:, 0:N4].rearrange("p (t h) -> p t h", h=4), op=ALU.subtract),
                      reads=["tmpg", ("ps", 1)], writes=["tmpg2"])
                    A("act", lambda e: e.activation(out=kwt[:], in_=tmpg2[:], func=AF.Exp, bias=LNC),
                      reads=["tmpg2"], writes=["kwt"])
                    for c in range(2):
                        A("act", lambda e, c=c: e.activation(out=f2(a01[c]), in_=pb[2 + c][:, 0:N4], func=AF.Exp, scale=-1.0),
                          reads=[("ps", 2 + c)], writes=[("a01", c)])
                        mc = cst_t[:, 640 + 128 * c:641 + 128 * c]
                        A("dve", lambda e, c=c, mc=mc: e.tensor_scalar(kw01[c][:], kwt[:], mc, None, ALU.mult),
                          reads=["kwt", "cst"], writes=[("kw01", c)])
                    A("dve", lambda e: e.memset(C32[:], 0.0), writes=[("C32", h) for h in range(4)])
                    A("pool", lambda e: e.memset(Cb[0][:], 0.0), writes=[("Cb", 0, h) for h in range(4)])
                    gres = ["ebt", "ksc", "kwt", ("a01", 0), ("a01", 1)]
                    nstate = 0
                    fin_pending = []
                    order = list(range(NT)) if dr == 0 else list(range(NT - 1, -1, -1))
                    corder = (0, 1) if dr == 0 else (1, 0)
                    for it, t in enumerate(order):
                        s = it % 2
                        r0 = t * 128
                        stage[0] = 1
                        A("sp", lambda e, s=s, r0=r0: e.dma_start(
                            out=qT[s][:], in_=qbT_s[:, r0:r0 + 128].rearrange("(h d) q -> d h q", d=128)),
                          writes=[("qT", s)], dma=True)
                        A("sp", lambda e, s=s, r0=r0: e.dma_start(
                            out=kT[s][:], in_=kbT_s[:, r0:r0 + 128].rearrange("(h d) q -> d h q", d=128)),
                          writes=[("kT", s)], dma=True)
                        A("sp", lambda e, s=s, r0=r0: e.dma_start(
                            out=Vt[s][:], in_=vb_s[r0:r0 + 128, :].rearrange("p (h c) -> p h c", c=129)),
                          writes=[("Vt", s)], dma=True)
                        if dr == 1:
                            A("sp", lambda e, s=s, r0=r0: e.dma_start(out=hft[s][:], in_=hf_s[r0:r0 + 128, :]),
                              reads=[("hf_s", t)], writes=[("hft", s)], dma=True)
                            A("sp", lambda e, s=s, r0=r0: e.dma_start(out=obt[s][:], in_=ob_s[r0:r0 + 128, :]),
                              writes=[("obt", s)], dma=True)
                        stage[0] = 2
                        for h in range(4):
                            A("pe", lambda e, s=s, h=h: e.transpose(out=pkt[s][:, h * 128:(h + 1) * 128], in_=kT[s][:, h, :], identity=identb[:]),
                              reads=[("kT", s), "identb"], writes=[("ps", "kt", 0)])
                        for c in range(2):
                            A("dve", lambda e, s=s, t=t, c=c: e.tensor_tensor(
                                out=Kw[s][:, c, :, :], in0=pkt[s][:, 0:512].rearrange("p (h d) -> p h d", h=4),
                                in1=kw01[c][:, t, :].unsqueeze(2).to_broadcast([128, 4, 128]), op=ALU.mult),
                              reads=[("ps", "kt", 0), ("kw01", c)], writes=[("Kw", s, h) for h in range(4)])
                        while fin_pending:
                            fin_pending.pop(0)()
                        stage[0] = 3
                        for h in range(4):
                            for c in range(2):
                                A("pe", lambda e, s=s, h=h, c=c: e.matmul(
                                    pb[ubank(h, c)][:, uoff(h, c):uoff(h, c) + 129],
                                    lhsT=Kw[s][:, c, h, :], rhs=Vt[s][:, h, :],
                                    start=True, stop=True, skip_group_check=True),
                                  reads=[("Kw", s, h), ("Vt", s)], writes=[("ps", ubank(h, c))])
                        stage[0] = 4
                        pa = 2 + s
                        for h in range(4):
                            A("pe", lambda e, s=s, h=h, pa=pa: e.matmul(pb[pa][:, h * 128:(h + 1) * 128], lhsT=kT[s][:, h, :], rhs=qT[s][:, h, :],
                                                                        start=True, stop=True, skip_group_check=True),
                              reads=[("kT", s), ("qT", s)], writes=[("ps", pa)])
                        for h in range(4):
                            A("dve", lambda e, s=s, h=h, pa=pa, t=t, mask=mask: e.scalar_tensor_tensor(
                                out=ATm[s][:, h, :], in0=pb[pa][:, h * 128:(h + 1) * 128], scalar=ksc[:, t, h:h + 1], in1=mask,
                                op0=ALU.mult, op1=ALU.mult),
                              reads=[("ps", pa), "ksc", "cst"], writes=[("ATm", s, h)])
                        stage[0] = 5
                        st_in = nstate
                        for ci, c in enumerate(corder):
                            nstate += 1
                            for h in range(4):
                                A("dve", lambda e, h=h, c=c, t=t: e.scalar_tensor_tensor(
                                    out=C32[:, h, :], in0=C32[:, h, :], scalar=a01[c][:, t, h:h + 1],
                                    in1=pb[ubank(h, c)][:, uoff(h, c):uoff(h, c) + 129],
                                    op0=ALU.mult, op1=ALU.add),
                                  reads=[("C32", h), ("a01", c), ("ps", ubank(h, c))], writes=[("C32", h)])
                                A("act", lambda e, h=h, n=nstate: e.activation(out=Cb[n % 3][:, h, :], in_=C32[:, h, :], func=AF.Copy),
                                  reads=[("C32", h)], writes=[("Cb", nstate % 3, h)])
                        stage[0] = 6
                        for h in range(4):
                            pn = 4 + h // 2
                            col = (h % 2) * 129
                            A("pe", lambda e, s=s, h=h, pn=pn, col=col: e.matmul(
                                pb[pn][:, col:col + 129], lhsT=ATm[s][:, h, :], rhs=Vt[s][:, h, :],
                                start=(h % 2 == 0), stop=False, skip_group_check=True),
                              reads=[("ATm", s, h), ("Vt", s)], writes=[("ps", pn)])
                            for ci, c in enumerate(corder):
                                sn = (st_in + ci) % 3
                                A("pe", lambda e, s=s, h=h, pn=pn, col=col, c=c, sn=sn: e.matmul(
                                    pb[pn][c * 64:(c + 1) * 64, col:col + 129], lhsT=qT[s][:, h, c * 64:(c + 1) * 64], rhs=Cb[sn][:, h, :],
                                    start=False, stop=True, skip_group_check=True),
                                  reads=[("qT", s), ("Cb", sn, h)], writes=[("ps", pn)])
                        stage[0] = 7
                        for bk in range(2):
                            dv = pb[4 + bk][:, 0:258].rearrange("p (h c) -> p h c", c=129)[:, :, 128]
                            A("dve", lambda e, bk=bk, dv=dv, t=t: e.tensor_tensor(
                                out=dd[:, 2 * bk:2 * bk + 2], in0=dv, in1=ebt[:, t, 2 * bk:2 * bk + 2], op=ALU.mult),
                              reads=[("ps", 4 + bk), "ebt"], writes=[("dd", bk)])
                        A("dve", lambda e: e.scalar_tensor_tensor(out=rr[:], in0=dd[:], scalar=-1.0, in1=dd[:], op0=ALU.mult, op1=ALU.max),
                          reads=[("dd", 0), ("dd", 1)], writes=["rr"])
                        A("dve", lambda e: e.tensor_scalar(rr[:], rr[:], 1.0, None, ALU.max), reads=["rr"], writes=["rr"])
                        A("dve", lambda e: e.reciprocal(out=rr[:], in_=rr[:]), reads=["rr"], writes=["rr"])
                        A("dve", lambda e, t=t: e.tensor_tensor(out=scl[:], in0=rr[:], in1=ebt[:, t, :], op=ALU.mult),
                          reads=["rr", "ebt"], writes=["scl"])
                        for bk in range(2):
                            A("dve", lambda e, bk=bk, s=s: e.tensor_tensor(
                                out=hout[s][:, 2 * bk:2 * bk + 2, :],
                                in0=pb[4 + bk][:, 0:258].rearrange("p (h c) -> p h c", c=129)[:, :, 0:128],
                                in1=scl[:, 2 * bk:2 * bk + 2].unsqueeze(2).to_broadcast([128, 2, 128]), op=ALU.mult),
                              reads=[("ps", 4 + bk), "scl"], writes=[("hout", s, 2 * bk), ("hout", s, 2 * bk + 1)])
                        stage[0] = 8
                        hres = [("hout", s, h) for h in range(4)]
                        if dr == 0:
                            A("pool", lambda e, s=s, r0=r0: e.dma_start(out=hf_s[r0:r0 + 128, :], in_=hout[s][:].rearrange("p h d -> p (h d)")),
                              reads=hres, writes=[("hf_s", t)], dma=True)
                        else:
                            if hb_s is not None:
                                A("pool", lambda e, s=s, r0=r0: e.dma_start(out=hb_s[r0:r0 + 128, :], in_=hout[s][:].rearrange("p h d -> p (h d)")),
                                  reads=hres, dma=True)
                            A("pool", lambda e, s=s: e.tensor_tensor(out=hft[s][:], in0=hft[s][:], in1=hout[s][:].rearrange("p h d -> p (h d)"), op=ALU.add),
                              reads=hres + [("hft", s)], writes=[("hft", s)])
                            for h in range(4):
                                A("act", lambda e, s=s, h=h: e.activation(out=sqj[:], in_=hft[s][:, h * 128:(h + 1) * 128], func=AF.Square,
                                                                          accum_out=ssq[:, h:h + 1]),
                                  reads=[("hft", s)], writes=["sqj", ("ssq", h)])
                            A("act", lambda e: e.activation(out=rsn[:], in_=ssq[:], func=AF.Ln, scale=1.0 / 128, bias=EPS),
                              reads=[("ssq", h) for h in range(4)], writes=["rsn"])
                            A("act", lambda e: e.activation(out=rsn[:], in_=rsn[:], func=AF.Exp, scale=-0.5),
                              reads=["rsn"], writes=["rsn"])

                            def fin_b(s=s, r0=r0):
                                for h in range(4):
                                    A("dve", lambda e, s=s, h=h: e.scalar_tensor_tensor(
                                        out=hn[:, h * 128:(h + 1) * 128], in0=hft[s][:, h * 128:(h + 1) * 128], scalar=rsn[:, h:h + 1],
                                        in1=hg_t[:, h * 128:(h + 1) * 128], op0=ALU.mult, op1=ALU.mult),
                                      reads=[("hft", s), "rsn", "hg"], writes=[("hn", h)])
                                A("pool", lambda e, s=s: e.tensor_tensor(out=ocat[s][:], in0=hn[:], in1=obt[s][:], op=ALU.mult),
                                  reads=[("hn", h) for h in range(4)] + [("obt", s)], writes=[("ocat", s)])
                                A("pool", lambda e, s=s, r0=r0: e.dma_start(out=cat_s[r0:r0 + 128, 512:1024], in_=ocat[s][:]),
                                  reads=[("ocat", s)], dma=True)
                            fin_pending.append(fin_b)
                    while fin_pending:
                        fin_pending.pop(0)()
                Sx.emit("p2b")


        if "ptest" in phases:
            xT_in = P.inp("xT_in", [D, S])
            cat_in = P.inp("cat_in", [S, D])
            with ExitStack() as L:
                A = Sx.add
                tb = [sb(L, "tb%d" % i, [128, D]) for i in range(2)]
                tbb = [sb(L, "tbb%d" % i, [128, D], BF16) for i in range(2)]
                for c in range(8):
                    A("sp", lambda e, c=c: e.dma_start(out=xT_s[c * 128:(c + 1) * 128, :], in_=xT_in[c * 128:(c + 1) * 128, :]), dma=True)
                for t in range(S // 128):
                    b = t % 2
                    A("sp", lambda e, t=t, b=b: e.dma_start(out=tb[b][:], in_=cat_in[t * 128:(t + 1) * 128, :]), writes=[("tb", b)], dma=True)
                    A("dve", lambda e, b=b: e.tensor_copy(out=tbb[b][:], in_=tb[b][:]), reads=[("tb", b)], writes=[("tbb", b)])
                    A("sp", lambda e, t=t, b=b: e.dma_start(out=cat_s[t * 128:(t + 1) * 128, :], in_=tbb[b][:]), reads=[("tbb", b)], dma=True)
                catTa_in = P.inp("catTa_in", [512, S])
                for c in range(4):
                    b = c % 2
                    for t0 in range(0, S, 1024):
                        A("sp", lambda e, c=c, b=b, t0=t0: e.dma_start(out=tb[b][:], in_=catTa_in[c * 128:(c + 1) * 128, t0:t0 + 1024]), writes=[("tb", b)], dma=True)
                        A("dve", lambda e, b=b: e.tensor_copy(out=tbb[b][:], in_=tb[b][:]), reads=[("tb", b)], writes=[("tbb", b)])
                        A("sp", lambda e, c=c, b=b, t0=t0: e.dma_start(out=catTa_s[c * 128:(c + 1) * 128, t0:t0 + 1024], in_=tbb[b][:]), reads=[("tbb", b)], dma=True)
                Sx.emit("ptest")

        _stg = {}

        def load_w(A, L, name, dst, src, nchunk, ncols, gain_col=None, vt=None, SW=1024):
            if id(L) not in _stg:
                _stg[id(L)] = [sb(L, "wstg%d" % i, [128, SW]) for i in range(2)]
            stg = _stg[id(L)]
            name_stg = "wstg"
            k = 0
            for c in range(nchunk):
                for c0 in range(0, ncols, SW):
                    n = min(SW, ncols - c0)
                    b = k % 2
                    k += 1
                    A("sp", lambda e, b=b, c=c, c0=c0, n=n: e.dma_start(out=stg[b][:, 0:n], in_=src[c * 128:(c + 1) * 128, c0:c0 + n]),
                      writes=[(name_stg, b)], dma=True)
                    eng = "dve" if b == 0 else "pool"
                    if gain_col is None:
                        A(eng, lambda e, b=b, c=c, c0=c0, n=n: e.tensor_copy(out=dst[:, c, c0:c0 + n], in_=stg[b][:, 0:n]),
                          reads=[(name_stg, b)], writes=[(name, b)])
                    else:
                        A(eng, lambda e, b=b, c=c, c0=c0, n=n: e.tensor_scalar(dst[:, c, c0:c0 + n], stg[b][:, 0:n],
                                                                             vt[:, gain_col + c:gain_col + c + 1], 1.0, ALU.mult, ALU.mult),
                          reads=[(name_stg, b), "vec3"], writes=[(name, b)])

        def rmsnorm_fm(A, xw, xn, sqt, lnt_, rstd_, ps_stat, W):
            for c in range(8):
                b = c % 2
                A("act", lambda e, c=c, b=b: e.activation(out=sqt[b][:, 0:W], in_=xw[:, c, 0:W], func=AF.Square),
                  reads=[("xw", c)], writes=[("sqt", b)])
                A("pe", lambda e, c=c, b=b: e.matmul(ps_stat[:, 0:W], lhsT=onesb, rhs=sqt[b][:, 0:W], start=(c == 0), stop=(c == 7)),
                  reads=[("sqt", b), "cstb"], writes=[("ps", "stat")])
            A("act", lambda e: e.activation(out=lnt_[:, 0:W], in_=ps_stat[:, 0:W], func=AF.Ln, scale=1.0 / D, bias=EPS),
              reads=[("ps", "stat")], writes=["lnt"])
            A("act", lambda e: e.activation(out=rstd_[:, 0:W], in_=lnt_[:, 0:W], func=AF.Exp, scale=-0.5),
              reads=["lnt"], writes=["rstd"])
            for c in range(8):
                eng = "dve" if c % 2 == 0 else "pool"
                A(eng, lambda e, c=c: e.tensor_tensor(out=xn[:, c, 0:W], in0=xw[:, c, 0:W], in1=rstd_[:, 0:W], op=ALU.mult),
                  reads=[("xw", c), "rstd"], writes=[("xn", c)])

        def load_xwin(A, xw, src, tok0, W):
            j0 = max(0, -tok0)
            j1 = min(W, S - tok0)
            res = [("xw", c) for c in range(8)]
            if j0 > 0:
                A("pool", lambda e: e.memset(xw[:, :, 0:j0], 0.0), writes=res)
            if j1 < W:
                A("pool", lambda e: e.memset(xw[:, :, j1:W], 0.0), writes=res)
            A("sp", lambda e: e.dma_start(out=xw[:, :, j0:j1], in_=src[:, tok0 + j0:tok0 + j1].rearrange("(c p) t -> p c t", p=128)),
              writes=res, dma=True)
            return j0, j1

        if "p3a" in phases:
            with ExitStack() as L:
                A = Sx.add
                pb = [L.enter_context(nc.psum_tensor(uq("pb%d" % i), [128, 512], F32)) for i in range(4)]
                pkt = [L.enter_context(nc.psum_tensor(uq("pkt%d" % i), [128, 1024], BF16)) for i in range(2)]
                Wo = sb(L, "Wo", [128, 8, D], BF16)
                identb = sb(L, "identb", [128, 128], BF16)
                ct4 = [sb(L, "ct4_%d" % i, [128, 4, 512], BF16) for i in range(2)]
                catT = [sb(L, "catT%d" % i, [128, 8, 512], BF16) for i in range(2)]
                xw2 = [sb(L, "xw%d" % i, [128, 8, 512]) for i in range(2)]
                A("sp", lambda e: e.dma_start(out=cst_t[:], in_=cst[:, :]), writes=["cst"], dma=True)
                A("dve", lambda e: e.tensor_copy(out=identb[:], in_=cst_t[:, 0:128]), reads=["cst"], writes=["identb"])
                load_wb(A, "Wo", Wo, wo_b, D)
                for w in range(S // 512):
                    a = w * 512
                    s = w % 2
                    A("sp", lambda e, a=a, s=s: e.dma_start(out=ct4[s][:], in_=cat_s[a:a + 512, 512:1024].rearrange("(j p) d -> p j d", p=128)),
                      writes=[("ct4", s)], dma=True)
                    A("sp", lambda e, a=a, s=s: e.dma_start(out=catT[s][:, 0:4, :], in_=catTa_s[:, a:a + 512].rearrange("(c p) t -> p c t", p=128)),
                      writes=[("catT", s, c) for c in range(4)], dma=True)
                    A("sp", lambda e, a=a, s=s: e.dma_start(out=xw2[s][:], in_=xT_s[:, a:a + 512].rearrange("(c p) t -> p c t", p=128)),
                      writes=[("xw", s, c) for c in range(8)], dma=True)
                    for c in range(4, 8):
                        k = c % 2
                        for j in range(4):
                            A("pe", lambda e, k=k, j=j, c=c, s=s: e.transpose(out=pkt[k][:, j * 128:(j + 1) * 128], in_=ct4[s][:, j, (c - 4) * 128:(c - 3) * 128],
                                                                              identity=identb[:]),
                              reads=[("ct4", s), "identb"], writes=[("ps", "kt", k)])
                        if c % 2 == 0:
                            A("act", lambda e, k=k, c=c, s=s: e.activation(out=catT[s][:, c, :], in_=pkt[k][:, 0:512], func=AF.Copy),
                              reads=[("ps", "kt", k)], writes=[("catT", s, c)])
                        else:
                            A("dve", lambda e, k=k, c=c, s=s: e.tensor_copy(out=catT[s][:, c, :], in_=pkt[k][:, 0:512]),
                              reads=[("ps", "kt", k)], writes=[("catT", s, c)])
                    for oc in range(8):
                        pi = oc % 4
                        for c in range(8):
                            A("pe", lambda e, pi=pi, c=c, oc=oc, s=s: e.matmul(pb[pi][:], lhsT=Wo[:, c, oc * 128:(oc + 1) * 128], rhs=catT[s][:, c, :],
                                                                              start=(c == 0), stop=(c == 7)),
                              reads=[("catT", s, c), ("Wo", oc // 4)], writes=[("ps", pi)])
                        A("dve", lambda e, pi=pi, oc=oc, s=s: e.tensor_tensor(out=xw2[s][:, oc, :], in0=pb[pi][:], in1=xw2[s][:, oc, :], op=ALU.add),
                          reads=[("ps", pi), ("xw", s, oc)], writes=[("xw", s, oc)])
                    A("pool", lambda e, a=a, s=s: e.dma_start(out=xB_s[:, a:a + 512].rearrange("(c p) t -> p c t", p=128), in_=xw2[s][:]),
                      reads=[("xw", s, c) for c in range(8)], dma=True)
                Sx.emit("p3a")

        def ffn_phase(layer, src, dst):
            WF = 384
            OUTW = WF - 2
            NWIN = (S + OUTW - 1) // OUTW
            with ExitStack() as L:
                A = Sx.add
                pb = [L.enter_context(nc.psum_tensor(uq("pb%d" % i), [128, 512], F32)) for i in range(7)]
                v3 = sb(L, "v3", [128, 704])
                Wu = sb(L, "Wu", [128, 8, 2 * D_FF], BF16)
                Wd = sb(L, "Wd", [128, 22, D], BF16)
                xw = sb(L, "xwf", [128, 8, WF])
                xnb = [sb(L, "xnf%d" % i, [128, 8, WF], BF16) for i in range(2)]
                sqt = [sb(L, "sqt%d" % i, [128, WF], BF16) for i in range(2)]
                lnt_ = sb(L, "lntf", [128, WF])
                rstd_ = sb(L, "rstdf", [128, WF])
                hh = sb(L, "hh", [128, 22, WF], BF16)
                og = [sb(L, "og%d" % i, [128, WF]) for i in range(2)]
                ov = [sb(L, "ov%d" % i, [128, WF]) for i in range(2)]
                xres = [sb(L, "xres%d" % i, [128, WF]) for i in range(2)]
                ost = [sb(L, "ost%d" % i, [128, WF]) for i in range(2)]
                load_cst(A)
                A("sp", lambda e: e.dma_start(out=v3[:], in_=vec3[:, :]), writes=["vec3"], dma=True)
                uorder = []
                for ct in range(22):
                    for p in (ct // 4, (22 + ct) // 4):
                        if p not in uorder:
                            uorder.append(p)
                load_wb(A, "Wu", Wu, wu_b[layer], 2 * D_FF, order=uorder[:3])

                def rest_weights():
                    load_wb(A, "Wu", Wu, wu_b[layer], 2 * D_FF, order=uorder[3:])
                    for ct in range(22):
                        A("sp", lambda e, ct=ct: e.dma_start(out=Wd[:, ct, :], in_=wd_b[layer][ct * 128:(ct + 1) * 128, :]),
                          writes=[("Wd", ct)], dma=True)
                A("pool", lambda e: e.memset(hh[:], 0.0), writes=[("hh", ct) for ct in range(22)])
                wcol = V3_WFF[layer]
                bcol = V3_BFF[layer]

                def norm(wi):
                    a = wi * OUTW
                    xn = xnb[wi % 2]
                    load_xwin(A, xw, src, a - 1, WF)
                    for c in range(8):
                        b = c % 2
                        A("act", lambda e, c=c, b=b: e.activation(out=sqt[b][:], in_=xw[:, c, :], func=AF.Square),
                          reads=[("xw", c)], writes=[("sqt", b)])
                        A("pe", lambda e, c=c, b=b: e.matmul(pb[6][:, 0:WF], lhsT=onesb, rhs=sqt[b][:], start=(c == 0), stop=(c == 7)),
                          reads=[("sqt", b), "cstb"], writes=[("ps", 6)])
                    A("act", lambda e: e.activation(out=lnt_[:], in_=pb[6][:, 0:WF], func=AF.Ln, scale=1.0 / D, bias=EPS),
                      reads=[("ps", 6)], writes=["lnt"])
                    A("act", lambda e: e.activation(out=rstd_[:], in_=lnt_[:], func=AF.Exp, scale=-0.5),
                      reads=["lnt"], writes=["rstd"])
                    for c in range(8):
                        eng = "dve" if c % 2 == 0 else "pool"
                        A(eng, lambda e, c=c, xn=xn: e.tensor_tensor(out=xn[:, c, :], in0=xw[:, c, :], in1=rstd_[:], op=ALU.mult),
                          reads=[("xw", c), "rstd"], writes=[("xn", wi % 2, c)])

                def up(wi):
                    xn = xnb[wi % 2]
                    n = WF - 2
                    for ct in range(22):
                        b = ct % 2
                        for half in range(2):
                            col = half * 22 + ct
                            pi = b * 2 + half
                            for c in range(8):
                                A("pe", lambda e, pi=pi, c=c, col=col, xn=xn: e.matmul(pb[pi][:, 0:WF], lhsT=Wu[:, c, col * 128:(col + 1) * 128], rhs=xn[:, c, :],
                                                                                   start=(c == 0), stop=(c == 7)),
                                  reads=[("xn", wi % 2, c), ("Wu", col // 4)], writes=[("ps", pi)])
                            o = og[b] if half == 0 else ov[b]
                            okey = ("og", b) if half == 0 else ("ov", b)
                            wk = lambda k, col=col: v3[:, wcol + col * 3 + k:wcol + col * 3 + k + 1]
                            bk = v3[:, bcol + col:bcol + col + 1]
                            A("act", lambda e, pi=pi, o=o, wk=wk, bk=bk: e.activation(out=o[:, 0:n], in_=pb[pi][:, 0:n], func=AF.Identity, scale=wk(0), bias=bk),
                              reads=[("ps", pi), "vec3"], writes=[okey])
                            A("dve", lambda e, pi=pi, o=o, wk=wk: e.scalar_tensor_tensor(out=o[:, 0:n], in0=pb[pi][:, 1:1 + n], scalar=wk(1), in1=o[:, 0:n],
                                                                                   op0=ALU.mult, op1=ALU.add),
                              reads=[("ps", pi), okey, "vec3"], writes=[okey])
                            A("dve", lambda e, pi=pi, o=o, wk=wk: e.scalar_tensor_tensor(out=o[:, 0:n], in0=pb[pi][:, 2:2 + n], scalar=wk(2), in1=o[:, 0:n],
                                                                                   op0=ALU.mult, op1=ALU.add),
                              reads=[("ps", pi), okey, "vec3"], writes=[okey])
                        A("act", lambda e, b=b: e.activation(out=og[b][:, 0:n], in_=og[b][:, 0:n], func=AF.Silu),
                          reads=[("og", b)], writes=[("og", b)])
                        A("pool", lambda e, b=b, ct=ct: e.tensor_tensor(out=hh[:, ct, 1:1 + n], in0=og[b][:, 0:n], in1=ov[b][:, 0:n], op=ALU.mult),
                          reads=[("og", b), ("ov", b)], writes=[("hh", ct)])

                def down(wi):
                    a = wi * OUTW
                    nout = min(OUTW, S - a)
                    for oc in range(8):
                        pi = 4 + oc % 2
                        r = oc % 2
                        A("sp", lambda e, oc=oc, r=r, a=a, nout=nout: e.dma_start(out=xres[r][:, 1:1 + nout], in_=src[oc * 128:(oc + 1) * 128, a:a + nout]),
                          writes=[("xres", r)], dma=True)
                        for ct in range(22):
                            A("pe", lambda e, pi=pi, ct=ct, oc=oc: e.matmul(pb[pi][:, 0:WF], lhsT=Wd[:, ct, oc * 128:(oc + 1) * 128], rhs=hh[:, ct, :],
                                                                          start=(ct == 0), stop=(ct == 21)),
                              reads=[("hh", ct), ("Wd", ct)], writes=[("ps", pi)])
                        A("dve", lambda e, pi=pi, r=r, nout=nout: e.tensor_tensor(out=ost[r][:, 1:1 + nout], in0=pb[pi][:, 1:1 + nout], in1=xres[r][:, 1:1 + nout], op=ALU.add),
                          reads=[("ps", pi), ("xres", r)], writes=[("ost", r)])
                        A("pool", lambda e, oc=oc, r=r, a=a, nout=nout: e.dma_start(out=dst[oc * 128:(oc + 1) * 128, a:a + nout], in_=ost[r][:, 1:1 + nout]),
                          reads=[("ost", r)], dma=True)

                norm(0)
                rest_weights()
                for wi in range(NWIN):
                    up(wi)
                    if wi + 1 < NWIN:
                        norm(wi + 1)
                    down(wi)
                Sx.emit("ffn%d" % layer)

        if "p3b" in phases:
            ffn_phase(0, xB_s, xT_s)

        if "p3c" in phases:
            WC = 512
            OUTW = WC - 30
            NWIN = (S + OUTW - 1) // OUTW
            with ExitStack() as L:
                A = Sx.add
                pb = [L.enter_context(nc.psum_tensor(uq("pb%d" % i), [128, 512], F32)) for i in range(8)]
                v3 = sb(L, "v3", [128, 704])
                W1 = sb(L, "W1c", [128, 8, 2 * D], BF16)
                W2 = sb(L, "W2c", [128, 8, D], BF16)
                identb = sb(L, "identb", [128, 128], BF16)
                dg = sb(L, "dg", [128, 8, 31, 128], BF16)
                xw = sb(L, "xwc", [128, 8, WC])
                xn = sb(L, "xnc", [128, 8, WC], BF16)
                sqt = [sb(L, "sqt%d" % i, [128, WC], BF16) for i in range(2)]
                lnt_ = sb(L, "lntc", [128, WC])
                rstd_ = sb(L, "rstdc", [128, WC])
                sg = [sb(L, "sg%d" % i, [128, WC]) for i in range(2)]
                glu = sb(L, "glu", [128, 8, WC], BF16)
                cvt = sb(L, "cvt", [128, 8, OUTW])
                acc = [sb(L, "cacc%d" % i, [128, OUTW]) for i in range(2)]
                sgm = [sb(L, "sgm%d" % i, [128, OUTW]) for i in range(2)]
                mean = sb(L, "mean", [128, OUTW])
                msq = sb(L, "msq", [128, OUTW])
                sn = sb(L, "snc", [128, 8, OUTW], BF16)
                load_cst(A)
                A("sp", lambda e: e.dma_start(out=v3[:], in_=vec3[:, :]), writes=["vec3"], dma=True)
                A("dve", lambda e: e.tensor_copy(out=identb[:], in_=cst_t[:, 0:128]), reads=["cst"], writes=["identb"])
                load_wb(A, "W1c", W1, w1_b, 2 * D, order=[0, 2, 1, 3])
                load_wb(A, "W2c", W2, w2_b, D)
                for ct in range(8):
                    for k in range(31):
                        eng = "dve" if (k % 2 == 0) else "pool"
                        A(eng, lambda e, ct=ct, k=k: e.tensor_scalar(dg[:, ct, k, :], identb[:], v3[:, V3_WDC + ct * 31 + k:V3_WDC + ct * 31 + k + 1],
                                                                   1.0, ALU.mult, ALU.mult),
                          reads=["identb", "vec3"], writes=[("dg", ct, k % 2)])
                pctr = [0]

                def nps():
                    pctr[0] += 1
                    return pctr[0] % 7
                xres = [sb(L, "xresc%d" % i, [128, OUTW]) for i in range(2)]
                ost = [sb(L, "ostc%d" % i, [128, OUTW]) for i in range(2)]
                jj = {}

                def c_load(wi):
                    jj[wi] = load_xwin(A, xw, xT_s, wi * OUTW - 15, WC)

                def c_norm(wi):
                    rmsnorm_fm(A, xw, xn, sqt, lnt_, rstd_, pb[7], WC)

                def c_pw1(wi, ct):
                    j0, j1 = jj[wi]
                    b = ct % 2
                    pa = nps()
                    for c in range(8):
                        A("pe", lambda e, pa=pa, c=c, ct=ct: e.matmul(pb[pa][:], lhsT=W1[:, c, ct * 128:(ct + 1) * 128], rhs=xn[:, c, :],
                                                                    start=(c == 0), stop=(c == 7)),
                          reads=[("xn", c), ("W1c", ct // 4)], writes=[("ps", pa)])
                    pg = nps()
                    for c in range(8):
                        A("pe", lambda e, pg=pg, c=c, ct=ct: e.matmul(pb[pg][:], lhsT=W1[:, c, D + ct * 128:D + (ct + 1) * 128], rhs=xn[:, c, :],
                                                                    start=(c == 0), stop=(c == 7)),
                          reads=[("xn", c), ("W1c", 2 + ct // 4)], writes=[("ps", pg)])
                    A("act", lambda e, pg=pg, b=b, ct=ct: e.activation(out=sg[b][:], in_=pb[pg][:], func=AF.Sigmoid,
                                                                     bias=v3[:, V3_BPW1 + 8 + ct:V3_BPW1 + 9 + ct]),
                      reads=[("ps", pg), "vec3"], writes=[("sg", b)])
                    A("dve", lambda e, pa=pa, b=b, ct=ct: e.scalar_tensor_tensor(out=glu[:, ct, :], in0=pb[pa][:], scalar=v3[:, V3_BPW1 + ct:V3_BPW1 + ct + 1],
                                                                               in1=sg[b][:], op0=ALU.add, op1=ALU.mult),
                      reads=[("ps", pa), ("sg", b), "vec3"], writes=[("glu", ct)])
                    if j0 > 0:
                        A("pool", lambda e, ct=ct, j0=j0: e.memset(glu[:, ct, 0:j0], 0.0), writes=[("glu", ct)])
                    if j1 < WC:
                        A("pool", lambda e, ct=ct, j1=j1: e.memset(glu[:, ct, j1:WC], 0.0), writes=[("glu", ct)])

                NDV = 6

                def c_conv(wi, ct):
                    pc = nps()
                    ab = ct % 2
                    wk = lambda k: v3[:, V3_WDC + ct * 31 + k:V3_WDC + ct * 31 + k + 1]
                    for k in range(NDV, 31):
                        A("pe", lambda e, pc=pc, ct=ct, k=k: e.matmul(pb[pc][:, 0:OUTW], lhsT=dg[:, ct, k, :], rhs=glu[:, ct, k:k + OUTW],
                                                                    start=(k == NDV), stop=(k == 30)),
                          reads=[("glu", ct), ("dg", ct, k % 2)], writes=[("ps", pc)])
                    A("dve", lambda e, ct=ct, ab=ab, wk=wk: e.tensor_scalar(acc[ab][:], glu[:, ct, 0:OUTW], wk(0), v3[:, V3_BDC + ct:V3_BDC + ct + 1],
                                                                         ALU.mult, ALU.add),
                      reads=[("glu", ct), "vec3"], writes=[("acc", ab)])
                    for k in range(1, NDV):
                        A("dve", lambda e, ct=ct, ab=ab, k=k, wk=wk: e.scalar_tensor_tensor(out=acc[ab][:], in0=glu[:, ct, k:k + OUTW], scalar=wk(k), in1=acc[ab][:],
                                                                                      op0=ALU.mult, op1=ALU.add),
                          reads=[("glu", ct), ("acc", ab), "vec3"], writes=[("acc", ab)])
                    A("dve", lambda e, pc=pc, ct=ct, ab=ab: e.tensor_tensor(out=cvt[:, ct, :], in0=pb[pc][:, 0:OUTW], in1=acc[ab][:], op=ALU.add),
                      reads=[("ps", pc), ("acc", ab)], writes=[("cvt", ct)])

                def c_lnstats(wi):
                    p1 = nps()
                    for ct in range(8):
                        A("pe", lambda e, p1=p1, ct=ct: e.matmul(pb[p1][:, 0:OUTW], lhsT=ones, rhs=cvt[:, ct, :], start=(ct == 0), stop=(ct == 7)),
                          reads=[("cvt", ct)], writes=[("ps", p1)])
                    p2 = nps()
                    for ct in range(8):
                        b = ct % 2
                        A("act", lambda e, ct=ct, b=b: e.activation(out=sqt[b][:, 0:OUTW], in_=cvt[:, ct, :], func=AF.Square),
                          reads=[("cvt", ct)], writes=[("sqt", b)])
                        A("pe", lambda e, p2=p2, ct=ct, b=b: e.matmul(pb[p2][:, 0:OUTW], lhsT=onesb, rhs=sqt[b][:, 0:OUTW], start=(ct == 0), stop=(ct == 7)),
                          reads=[("sqt", b), "cstb"], writes=[("ps", p2)])
                    A("act", lambda e, p1=p1: e.activation(out=mean[:], in_=pb[p1][:, 0:OUTW], func=AF.Copy, scale=1.0 / D),
                      reads=[("ps", p1)], writes=["mean"])
                    A("act", lambda e: e.activation(out=msq[:], in_=mean[:], func=AF.Square), reads=["mean"], writes=["msq"])
                    A("dve", lambda e, p2=p2: e.scalar_tensor_tensor(out=msq[:], in0=pb[p2][:, 0:OUTW], scalar=1.0 / D, in1=msq[:],
                                                                    op0=ALU.mult, op1=ALU.subtract),
                      reads=[("ps", p2), "msq"], writes=["msq"])
                    A("act", lambda e: e.activation(out=msq[:], in_=msq[:], func=AF.Ln, bias=EPS), reads=["msq"], writes=["msq"])
                    A("act", lambda e: e.activation(out=msq[:], in_=msq[:], func=AF.Exp, scale=-0.5), reads=["msq"], writes=["msq"])

                def c_normalize(wi, ct):
                    eng = "dve" if ct % 2 == 0 else "pool"
                    sb_ = ct % 2
                    gcol = v3[:, V3_LNG + ct:V3_LNG + ct + 1]
                    bcol = v3[:, V3_LNB + ct:V3_LNB + ct + 1]
                    A(eng, lambda e, ct=ct: e.tensor_tensor(out=cvt[:, ct, :], in0=cvt[:, ct, :], in1=mean[:], op=ALU.subtract),
                      reads=[("cvt", ct), "mean"], writes=[("cvt", ct)])
                    A(eng, lambda e, ct=ct: e.tensor_tensor(out=cvt[:, ct, :], in0=cvt[:, ct, :], in1=msq[:], op=ALU.mult),
                      reads=[("cvt", ct), "msq"], writes=[("cvt", ct)])
                    A("act", lambda e, ct=ct, sb_=sb_, gcol=gcol, bcol=bcol: e.activation(out=sgm[sb_][:], in_=cvt[:, ct, :], func=AF.Sigmoid, scale=gcol, bias=bcol),
                      reads=[("cvt", ct), "vec3"], writes=[("sgm", sb_)])
                    A(eng, lambda e, ct=ct, gcol=gcol, bcol=bcol: e.tensor_scalar(cvt[:, ct, :], cvt[:, ct, :], gcol, bcol, ALU.mult, ALU.add),
                      reads=[("cvt", ct), "vec3", ("sgm", sb_)], writes=[("cvt", ct)])
                    A(eng, lambda e, ct=ct, sb_=sb_: e.tensor_tensor(out=sn[:, ct, :], in0=cvt[:, ct, :], in1=sgm[sb_][:], op=ALU.mult),
                      reads=[("cvt", ct), ("sgm", sb_)], writes=[("sn", ct)])

                def c_pw2(wi):
                    a = wi * OUTW
                    nout = min(OUTW, S - a)
                    for oc in range(8):
                        r = oc % 2
                        A("sp", lambda e, oc=oc, r=r, a=a, nout=nout: e.dma_start(out=xres[r][:, 0:nout], in_=xT_s[oc * 128:(oc + 1) * 128, a:a + nout]),
                          writes=[("xres", r)], dma=True)
                        po = nps()
                        for c in range(8):
                            A("pe", lambda e, po=po, c=c, oc=oc: e.matmul(pb[po][:, 0:OUTW], lhsT=W2[:, c, oc * 128:(oc + 1) * 128], rhs=sn[:, c, :],
                                                                        start=(c == 0), stop=(c == 7)),
                              reads=[("sn", c), ("W2c", oc // 4)], writes=[("ps", po)])
                        A("dve", lambda e, po=po, oc=oc, r=r, nout=nout: e.scalar_tensor_tensor(
                            out=ost[r][:, 0:nout], in0=pb[po][:, 0:nout], scalar=v3[:, V3_BPW2 + oc:V3_BPW2 + oc + 1], in1=xres[r][:, 0:nout],
                            op0=ALU.add, op1=ALU.add),
                          reads=[("ps", po), ("xres", r), "vec3"], writes=[("ost", r)])
                        A("pool", lambda e, oc=oc, r=r, a=a, nout=nout: e.dma_start(out=xB_s[oc * 128:(oc + 1) * 128, a:a + nout], in_=ost[r][:, 0:nout]),
                          reads=[("ost", r)], dma=True)

                c_load(0)
                c_norm(0)
                for ct in range(8):
                    c_pw1(0, ct)
                for wi in range(NWIN):
                    more = wi + 1 < NWIN
                    if more:
                        c_load(wi + 1)
                    for ct in range(4):
                        c_conv(wi, ct)
                    if more:
                        c_norm(wi + 1)
                    for ct in range(4, 8):
                        c_conv(wi, ct)
                    c_lnstats(wi)
                    for ct in range(8):
                        if more:
                            c_pw1(wi + 1, ct)
                        c_normalize(wi, ct)
                    c_pw2(wi)
                Sx.emit("p3c")

        if "p3d" in phases:
            ffn_phase(1, xB_s, xT_s)

        if "p3e" in phases:
            with ExitStack() as L:
                A = Sx.add
                pb = [L.enter_context(nc.psum_tensor(uq("pb%d" % i), [128, 512], F32)) for i in range(4)]
                xw2 = [sb(L, "xwe%d" % i, [128, 8, 512]) for i in range(2)]
                yt = [sb(L, "yt%d" % i, [128, D]) for i in range(3)]
                A("sp", lambda e: e.dma_start(out=cst_t[:], in_=cst[:, :]), writes=["cst"], dma=True)
                k = 0
                for w in range(S // 512):
                    a = w * 512
                    s = w % 2
                    A("sp", lambda e, a=a, s=s: e.dma_start(out=xw2[s][:], in_=xT_s[:, a:a + 512].rearrange("(c p) t -> p c t", p=128)),
                      writes=[("xw", s)], dma=True)
                    for j in range(4):
                        yb = k % 3
                        for hf in range(2):
                            pi = (k * 2 + hf) % 4
                            for c4 in range(4):
                                c = hf * 4 + c4
                                A("pe", lambda e, pi=pi, c=c, c4=c4, j=j, s=s: e.transpose(out=pb[pi][:, c4 * 128:(c4 + 1) * 128],
                                                                                          in_=xw2[s][:, c, j * 128:(j + 1) * 128], identity=ident),
                                  reads=[("xw", s), "cst"], writes=[("ps", pi)])
                            if hf == 0:
                                A("act", lambda e, pi=pi, yb=yb: e.activation(out=yt[yb][:, 0:512], in_=pb[pi][:], func=AF.Copy),
                                  reads=[("ps", pi)], writes=[("yt", yb, 0)])
                            else:
                                A("dve", lambda e, pi=pi, yb=yb: e.tensor_copy(out=yt[yb][:, 512:1024], in_=pb[pi][:]),
                                  reads=[("ps", pi)], writes=[("yt", yb, 1)])
                        A("pool", lambda e, a=a, j=j, yb=yb: e.dma_start(out=y_d[a + j * 128:a + (j + 1) * 128, :], in_=yt[yb][:]),
                          reads=[("yt", yb, 0), ("yt", yb, 1)], dma=True)
                        k += 1
                Sx.emit("p3e")
    return P


def rope_tables(S):
    t = np.arange(S)
    row = (t // GRID_W).astype(np.float32)
    col = (t % GRID_W).astype(np.float32)
    inv = (np.float32(10000.0) ** (-np.arange(16, dtype=np.float32) / np.float32(16))).astype(np.float32)
    C = np.zeros((128, S), np.float32)
    Sg = np.zeros((128, S), np.float32)
    for p in range(128):
        d = p % 64
        sec, half, pair = d // 32, (d % 32) // 16, d % 16
        ang = (row if sec == 0 else col) * inv[pair]
        C[p] = np.cos(ang)
        Sg[p] = np.sin(ang) * (-1.0 if half == 0 else 1.0)
    return C, Sg


def rot_perm64():
    idx = np.arange(64)
    sec, half, pair = idx // 32, (idx % 32) // 16, idx % 16
    return sec * 32 + (1 - half) * 16 + pair


def host_prep(S, inputs):
    f = lambda a: np.ascontiguousarray(np.asarray(a, dtype=np.float32))
    w_in = f(inputs["w_in"][0])
    perm = rot_perm64()
    qa = w_in[:, 0:512]
    ka = w_in[:, 512:640]
    va = w_in[:, 640:768]
    qb = w_in[:, 768:1280]
    kb = w_in[:, 1280:1792]
    vb = w_in[:, 1792:2304]
    ob = w_in[:, 2304:2816]
    gt = w_in[:, 2816:2832]
    qar = qa.reshape(D, 8, 64)[:, :, perm].reshape(D, 512)
    kar = ka.reshape(D, 2, 64)[:, :, perm].reshape(D, 128)
    w_in_ext = np.concatenate([qa, qar, ka, kar, qb, kb, va, gt, vb, ob], axis=1)
    assert w_in_ext.shape[1] == NCOL_IN
    cst = np.zeros((128, 1024), np.float32)
    cst[:, 0:128] = np.eye(128, dtype=np.float32)
    cst[:, 128:256] = 1.0
    cst[0:64, 256:320] = 1.0
    cst[64:128, 320:384] = 1.0
    kk = np.arange(128)[:, None]
    ll = np.arange(128)[None, :]
    same = (kk // 64) == (ll // 64)
    cst[:, 384:512] = (same & (kk <= ll)).astype(np.float32)
    cst[:, 512:640] = (same & (kk >= ll)).astype(np.float32)
    cst[0:64, 640:768] = 1.0
    cst[64:128, 768:896] = 1.0
    C, Sg = rope_tables(S)
    vecs = np.zeros((128, 64), np.float32)
    vecs[:, 0:8] = f(inputs["mix_norm_e"][0]).reshape(8, 128).T
    gq = f(inputs["q_gain_a"][0])
    gk = f(inputs["k_gain_a"][0])
    vecs[:, 8] = np.tile(gq, 2)
    vecs[:, 9] = np.tile(gq[perm], 2)
    vecs[:, 10] = np.tile(gk, 2)
    vecs[:, 11] = np.tile(gk[perm], 2)
    wqk = f(inputs["w_qk_conv_b"][0])
    for ct in range(8):
        for k in range(3):
            vecs[:, 12 + ct * 3 + k] = wqk[k, ct * 128:(ct + 1) * 128]
    bgt = np.tile(f(inputs["b_gates_b"][0])[None, :], (128, 1))
    hgt = np.tile(f(inputs["h_gain_b"][0])[None, :], (128, 1))
    v3 = np.zeros((128, 704), np.float32)
    pc = lambda v: f(v).reshape(-1, 128).T
    v3[:, 0:8] = pc(inputs["ffn_norm"][0])
    v3[:, 8:16] = pc(inputs["ffn_norm"][1])
    v3[:, 16:24] = pc(inputs["mix_norm_o"][0])
    v3[:, 24:40] = pc(inputs["b_pw1_c"][0])
    wdc = f(inputs["w_dw_c"][0])
    for ct in range(8):
        v3[:, 40 + ct * 31:40 + (ct + 1) * 31] = wdc[:, ct * 128:(ct + 1) * 128].T
    v3[:, 288:296] = pc(inputs["b_dw_c"][0])
    v3[:, 296:304] = pc(inputs["ln_g_c"][0])
    v3[:, 304:312] = pc(inputs["ln_b_c"][0])
    v3[:, 312:320] = pc(inputs["b_pw2_c"][0])
    for l in range(2):
        wff = f(inputs["w_dw_ff"][l])
        base = (320, 452)[l]
        for ct in range(44):
            v3[:, base + ct * 3:base + ct * 3 + 3] = wff[:, ct * 128:(ct + 1) * 128].T
        v3[:, (584, 628)[l]:(584, 628)[l] + 44] = pc(inputs["b_dw_ff"][l])
    extra = dict(w_out=f(inputs["w_out_e"][0]), w_up0=f(inputs["w_up"][0]), w_up1=f(inputs["w_up"][1]),
                 w_down0=f(inputs["w_down"][0]), w_down1=f(inputs["w_down"][1]), w_pw1=f(inputs["w_pw1_c"][0]),
                 w_pw2=f(inputs["w_pw2_c"][0]), vec3=v3)
    return dict(w_in_ext=np.ascontiguousarray(w_in_ext), cst=cst, ropeC=C, ropeS=Sg, vecs=vecs, bgt=bgt, hgt=hgt, **extra)


_CACHE = {}


def kernel(**inputs):
    S = 8192
    shared = host_prep(S, inputs)
    xs = [np.asarray(inputs["x_prompt"][i], np.float32) for i in range(4)] + \
         [np.asarray(inputs["x_sample"][i], np.float32) for i in range(2)]
    busy = [0, 1, 2, 4, 5, 6]
    zero = np.zeros_like(xs[0])
    P = build(S, phases=("p1", "p2a", "p2b", "p3a", "p3b", "p3c", "p3d", "p3e"))
    in_maps = []
    for c in range(8):
        m = dict(shared)
        m["x"] = np.ascontiguousarray(xs[busy.index(c)]) if c in busy else zero
        in_maps.append(m)
    res = run_bass_kernel_spmd(P.nc, in_maps, core_ids=list(range(8)))
    ys = [np.asarray(res.results[c]["y"], np.float32) for c in busy]
    return (np.stack(ys[0:4], 0), np.stack(ys[4:6], 0))
```
